# Optimizing a Trainium2 kernel written in Bass

```python
import math
import jax, jax.numpy as jnp
from jax import lax
import numpy as np

D_MODEL = 1024
BATCH = 8
SEQ = 4096
DEPTH = 2
DEC_BATCH = 4
DEC_SEQ = 8192
PAST_LEN = 128

MIX_WIDTH = D_MODEL
ATTN_WIDTH = MIX_WIDTH // 2
HG_WIDTH = MIX_WIDTH - ATTN_WIDTH
ATTN_HEAD_DIM = 64
ATTN_HEADS = ATTN_WIDTH // (2 * ATTN_HEAD_DIM)
ROT_DIM = ATTN_HEAD_DIM // 4
ROPE_THETA = 500000.0
Q_BLOCK = 128
HG_HEAD_DIM = 128
HG_HEADS = HG_WIDTH // HG_HEAD_DIM
HG_CHUNK = 64
D_FF = ((8 * D_MODEL // 3 + 127) // 128) * 128
ALPHA = (2 * DEPTH) ** 0.25
BETA = (8 * DEPTH) ** -0.25
EPS = 1e-5
IN_WIDTH = 3 * ATTN_WIDTH + 5 * HG_WIDTH
SPLITS = [ATTN_WIDTH, 2 * ATTN_WIDTH, 3 * ATTN_WIDTH,
          3 * ATTN_WIDTH + HG_WIDTH, 3 * ATTN_WIDTH + 2 * HG_WIDTH,
          3 * ATTN_WIDTH + 3 * HG_WIDTH, 3 * ATTN_WIDTH + 4 * HG_WIDTH]

kernel_name = "hymba_diffattn_hgrn2_macaron_deepnorm_encoder"


def layer_norm(x, g, b):
    xf = x.astype(jnp.float32)
    mu = jnp.mean(xf, axis=-1, keepdims=True)
    var = jnp.mean(jnp.square(xf - mu), axis=-1, keepdims=True)
    y = (xf - mu) * lax.rsqrt(var + EPS) * g.astype(jnp.float32) + b.astype(jnp.float32)
    return y.astype(x.dtype)


def rms_norm(x, g):
    xf = x.astype(jnp.float32)
    y = xf * lax.rsqrt(jnp.mean(jnp.square(xf), axis=-1, keepdims=True) + EPS) * g.astype(jnp.float32)
    return y.astype(x.dtype)


def swiglu(x, w_gate, w_up, w_down):
    hg = jnp.einsum('bsd,df->bsf', x, w_gate)
    hu = jnp.einsum('bsd,df->bsf', x, w_up)
    return jnp.einsum('bsf,fd->bsd', jax.nn.silu(hg) * hu, w_down)


def apply_partial_rope(x, pos):
    half = ROT_DIM // 2
    inv_freq = jnp.float32(ROPE_THETA) ** (-jnp.arange(half, dtype=jnp.float32) * 2.0 / ROT_DIM)
    ang = pos[:, None] * inv_freq[None, :]
    cos = jnp.cos(ang)[None, :, None, None, :]
    sin = jnp.sin(ang)[None, :, None, None, :]
    xr = x[..., :ROT_DIM].astype(jnp.float32)
    x1, x2 = xr[..., :half], xr[..., half:]
    rot = jnp.concatenate([x1 * cos - x2 * sin, x2 * cos + x1 * sin], axis=-1).astype(x.dtype)
    return jnp.concatenate([rot, x[..., ROT_DIM:]], axis=-1)


def diff_attention(q, k, v, lam):
    B, S, H, _, Dh = q.shape
    nq = S // Q_BLOCK
    scale = 1.0 / math.sqrt(Dh)
    qb = q.reshape(B, nq, Q_BLOCK, H, 2, Dh).transpose(1, 0, 2, 3, 4, 5)

    def one_block(qblk):
        s = jnp.einsum('bqhcd,bkhcd->bhcqk', qblk, k).astype(jnp.float32) * scale
        p = jax.nn.softmax(s, axis=-1)
        p = p[:, :, 0] - lam * p[:, :, 1]
        return jnp.einsum('bhqk,bkhe->bqhe', p.astype(v.dtype), v)

    o = lax.map(one_block, qb)
    return o.transpose(1, 0, 2, 3, 4).reshape(B, S, H, v.shape[-1])


def hgrn2_chunk_scan(q, k, v, log_f):
    B, S, H, Dk = q.shape
    Dv = v.shape[-1]
    n = S // HG_CHUNK

    def to_chunks(x):
        return x.reshape(B, n, HG_CHUNK, H, x.shape[-1]).transpose(1, 0, 3, 2, 4)

    mask = jnp.tril(jnp.ones((HG_CHUNK, HG_CHUNK), dtype=bool))

    def step(state, inp):
        qc, kc, vc, gc = inp
        b = jnp.cumsum(gc, axis=2)
        rel = jnp.where(mask[:, :, None], b[:, :, :, None, :] - b[:, :, None, :, :], -jnp.inf)
        scores = jnp.einsum('bhtk,bhsk,bhtsk->bhts', qc, kc, jnp.exp(rel))
        o = (jnp.einsum('bhts,bhsv->bhtv', scores, vc)
             + jnp.einsum('bhtk,bhkv->bhtv', qc * jnp.exp(b), state))
        b_last = b[:, :, -1:, :]
        state = (state * jnp.exp(b_last)[:, :, 0, :, None]
                 + jnp.einsum('bhsk,bhsv->bhkv', kc * jnp.exp(b_last - b), vc))
        return state, o

    init = jnp.zeros((B, H, Dk, Dv), jnp.float32)
    _, o = lax.scan(step, init, (to_chunks(q), to_chunks(k), to_chunks(v), to_chunks(log_f)))
    return o.transpose(1, 0, 3, 2, 4).reshape(B, S, H, Dv)


def hgrn2_direction(q, f_logit, lb, v, reverse):
    B, S = f_logit.shape[:2]
    z = f_logit.astype(jnp.float32).reshape(B, S, HG_HEADS, HG_HEAD_DIM)
    lbh = lb.reshape(HG_HEADS, HG_HEAD_DIM)
    log_f = jnp.logaddexp(jnp.log(lbh), jnp.log1p(-lbh) + jax.nn.log_sigmoid(z))
    k = -jnp.expm1(log_f)
    if reverse:
        q, k, v, log_f = (jnp.flip(t, axis=1) for t in (q, k, v, log_f))
    o = hgrn2_chunk_scan(q, k, v, log_f)
    if reverse:
        o = jnp.flip(o, axis=1)
    return o


def token_mixer(h, w_in, w_out, lam_p, lam_init, attn_g, hg_g, lb_fwd, lb_bwd):
    B, S, _ = h.shape
    proj = jnp.einsum('bsd,de->bse', h, w_in)
    aq, ak, av, hq, hf_fwd, hf_bwd, hi, hgate = jnp.split(proj, SPLITS, axis=-1)

    pos = jnp.arange(S, dtype=jnp.float32)
    aq = apply_partial_rope(aq.reshape(B, S, ATTN_HEADS, 2, ATTN_HEAD_DIM), pos)
    ak = apply_partial_rope(ak.reshape(B, S, ATTN_HEADS, 2, ATTN_HEAD_DIM), pos)
    av = av.reshape(B, S, ATTN_HEADS, 2 * ATTN_HEAD_DIM)
    lp = lam_p.astype(jnp.float32)
    lam = jnp.exp(jnp.sum(lp[0] * lp[1])) - jnp.exp(jnp.sum(lp[2] * lp[3])) + lam_init
    ao = diff_attention(aq, ak, av, lam)
    ao = rms_norm(ao, attn_g) * (1.0 - lam_init)

    qf = jax.nn.silu(hq.astype(jnp.float32)).reshape(B, S, HG_HEADS, HG_HEAD_DIM)
    vf = hi.astype(jnp.float32).reshape(B, S, HG_HEADS, HG_HEAD_DIM)
    ho = (hgrn2_direction(qf, hf_fwd, lb_fwd, vf, False)
          + hgrn2_direction(qf, hf_bwd, lb_bwd, vf, True)).astype(h.dtype)
    ho = rms_norm(ho, hg_g) * jax.nn.silu(hgate).reshape(B, S, HG_HEADS, HG_HEAD_DIM)

    mixed = jnp.concatenate([ao.reshape(B, S, ATTN_WIDTH), ho.reshape(B, S, HG_WIDTH)], axis=-1)
    return jnp.einsum('bse,ed->bsd', mixed, w_out)


def trunk(x, w_in, w_out, attn_lambda, attn_norm_g, hg_norm_g, hg_lower_bound,
          ffn_w_gate, ffn_w_up, ffn_w_down, ln_g, ln_b):
    p = jax.nn.softmax(hg_lower_bound.astype(jnp.float32), axis=1)
    lbs = jnp.maximum(jnp.cumsum(p, axis=1) - p[:, :1], 0.0)
    for l in range(DEPTH):
        lam_init = 0.8 - 0.6 * math.exp(-0.3 * l)
        x = layer_norm(ALPHA * x + 0.5 * swiglu(x, ffn_w_gate[l, 0], ffn_w_up[l, 0], ffn_w_down[l, 0]),
                       ln_g[l, 0], ln_b[l, 0])
        x = layer_norm(ALPHA * x + token_mixer(x, w_in[l], w_out[l], attn_lambda[l], lam_init,
                                               attn_norm_g[l], hg_norm_g[l], lbs[0, l], lbs[1, l]),
                       ln_g[l, 1], ln_b[l, 1])
        x = layer_norm(ALPHA * x + 0.5 * swiglu(x, ffn_w_gate[l, 1], ffn_w_up[l, 1], ffn_w_down[l, 1]),
                       ln_g[l, 2], ln_b[l, 2])
    return x


def setup_inputs(seed: int = 0) -> dict:
    key = jax.random.key(seed)
    ks = jax.random.split(key, 14)
    f32 = jnp.float32
    x_prompt = jax.random.normal(ks[0], (BATCH, SEQ, D_MODEL), f32)
    x_sample = jax.random.normal(ks[1], (DEC_BATCH, DEC_SEQ, D_MODEL), f32)
    col_scale = jnp.concatenate([
        jnp.ones((2 * ATTN_WIDTH,), f32), jnp.full((ATTN_WIDTH,), BETA, f32),
        jnp.ones((3 * HG_WIDTH,), f32), jnp.full((HG_WIDTH,), BETA, f32),
        jnp.ones((HG_WIDTH,), f32)])
    w_in = jax.random.normal(ks[2], (DEPTH, D_MODEL, IN_WIDTH), f32) * (D_MODEL ** -0.5) * col_scale
    w_out = jax.random.normal(ks[3], (DEPTH, MIX_WIDTH, D_MODEL), f32) * (MIX_WIDTH ** -0.5) * BETA
    attn_lambda = jax.random.normal(ks[4], (DEPTH, 4, ATTN_HEAD_DIM), f32) * 0.1
    attn_norm_g = 1.0 + 0.02 * jax.random.normal(ks[5], (DEPTH, 2 * ATTN_HEAD_DIM), f32)
    hg_norm_g = 1.0 + 0.02 * jax.random.normal(ks[6], (DEPTH, HG_HEAD_DIM), f32)
    hg_lower_bound = jax.random.normal(ks[7], (2, DEPTH, HG_WIDTH), f32)
    ffn_w_gate = jax.random.normal(ks[8], (DEPTH, 2, D_MODEL, D_FF), f32) * (D_MODEL ** -0.5)
    ffn_w_up = jax.random.normal(ks[9], (DEPTH, 2, D_MODEL, D_FF), f32) * (D_MODEL ** -0.5)
    ffn_w_down = jax.random.normal(ks[10], (DEPTH, 2, D_FF, D_MODEL), f32) * (D_FF ** -0.5) * BETA
    ln_g = 1.0 + 0.02 * jax.random.normal(ks[11], (DEPTH, 3, D_MODEL), f32)
    ln_b = 0.02 * jax.random.normal(ks[12], (DEPTH, 3, D_MODEL), f32)
    return {"x_prompt": x_prompt, "x_sample": x_sample, "w_in": w_in, "w_out": w_out,
            "attn_lambda": attn_lambda, "attn_norm_g": attn_norm_g, "hg_norm_g": hg_norm_g,
            "hg_lower_bound": hg_lower_bound, "ffn_w_gate": ffn_w_gate, "ffn_w_up": ffn_w_up,
            "ffn_w_down": ffn_w_down, "ln_g": ln_g, "ln_b": ln_b}


def reference(x_prompt, x_sample, w_in, w_out, attn_lambda, attn_norm_g, hg_norm_g, hg_lower_bound,
              ffn_w_gate, ffn_w_up, ffn_w_down, ln_g, ln_b):
    y_prompt = trunk(x_prompt, w_in, w_out, attn_lambda, attn_norm_g, hg_norm_g, hg_lower_bound,
                     ffn_w_gate, ffn_w_up, ffn_w_down, ln_g, ln_b)
    y_sample = trunk(x_sample, w_in, w_out, attn_lambda, attn_norm_g, hg_norm_g, hg_lower_bound,
                     ffn_w_gate, ffn_w_up, ffn_w_down, ln_g, ln_b)
    return (y_prompt, y_sample)
```

```python
import math
from contextlib import ExitStack

import numpy as np
import concourse.bass as bass
import concourse.mybir as mybir
from concourse.bass_utils import run_bass_kernel_spmd

F32 = mybir.dt.float32
BF16 = mybir.dt.bfloat16
I32 = mybir.dt.int32
AF = mybir.ActivationFunctionType
ALU = mybir.AluOpType
AX = mybir.AxisListType

D = 1024
DFF = 2816
KC = 8
FC = 22
T = 512
EPS = 1e-5
ALPHA = 4.0 ** 0.25
THETA = 500000.0
LAM_INIT = [0.8 - 0.6 * math.exp(-0.3 * l) for l in range(2)]
NEG = -30000.0


class Sch:
    def __init__(self, nc, es):
        self.nc = nc
        self.es = es
        self.engs = {"pe": nc.tensor, "act": nc.scalar, "dve": nc.vector, "pool": nc.gpsimd, "sp": nc.sync}
        self.esem = {e: es.enter_context(nc.semaphore("sem_" + e)) for e in ("pe", "act", "dve", "pool")}
        self.ecnt = {e: 0 for e in self.esem}
        self.dsem = {}
        self.dcnt = {}
        self.waited = {e: {} for e in self.engs}
        self.ops = []
        self.base = 0
        self.sig = {}
        self.lastw = {}
        self.rd = {}
        self.dma_last = {}
        self.last_on = {}
        self.nsem = 4
        self.eng_of = {}

    def op(self, eng, fn, r=(), w=(), dkey=None):
        i = self.base + len(self.ops)
        deps = set()
        for t in r:
            j = self.lastw.get(t)
            if j is not None:
                deps.add(j)
        for t in w:
            j = self.lastw.get(t)
            if j is not None:
                deps.add(j)
            rr = self.rd.get(t)
            if rr:
                deps.update(rr[0].values())
                deps.update(rr[1])
        if dkey is not None:
            j = self.dma_last.get(dkey)
            if j is not None:
                deps.add(j)
            self.dma_last[dkey] = i
        for t in r:
            rr = self.rd.setdefault(t, ({}, []))
            if dkey is not None:
                rr[1].append(i)
            else:
                rr[0][eng] = i
        for t in w:
            self.lastw[t] = i
            self.rd[t] = ({}, [])
        deps.discard(i)
        self.ops.append((eng, fn, deps, dkey))
        self.eng_of[i] = eng if dkey is None else "dma"
        if dkey is None:
            self.last_on[eng] = i
        return i

    def barrier(self):
        deps = set(self.last_on.values()) | set(self.dma_last.values())
        for e in self.engs:
            self.ops.append((e, None, set(deps), None))
        self.lastw = {}
        self.rd = {}
        self.flush()

    def flush(self):
        nc = self.nc
        needed = set()
        for (eng_, fn_, deps, dk_) in self.ops:
            if eng_ == "pe" and dk_ is None and fn_ is not None:
                needed |= {d for d in deps if self.eng_of.get(d) != "pe"}
            else:
                needed |= deps
        eng_of = {}
        for k, (eng, fn, deps, dkey) in enumerate(self.ops):
            i = self.base + k
            E = self.engs[eng]
            w8 = self.waited[eng]
            for d in sorted(deps):
                if d not in self.sig:
                    continue
                sem, val, seng, key = self.sig[d]
                if seng == "pe" and eng == "pe":
                    continue
                if w8.get(key, 0) < val:
                    E.wait_ge(sem, val)
                    w8[key] = val
            if fn is None:
                continue
            ins = fn()
            if dkey is not None:
                if dkey not in self.dsem:
                    self.dsem[dkey] = self.es.enter_context(nc.semaphore("dsem%d" % len(self.dsem)))
                    self.dcnt[dkey] = 0
                    self.nsem += 1
                self.dcnt[dkey] += 16
                ins.then_inc(self.dsem[dkey], 16)
                self.sig[i] = (self.dsem[dkey], self.dcnt[dkey], "dma", ("d", dkey))
            elif i in needed:
                self.ecnt[eng] += 1
                ins.then_inc(self.esem[eng], 1)
                self.sig[i] = (self.esem[eng], self.ecnt[eng], eng, ("e", eng))
        self.base += len(self.ops)
        self.ops = []
        if not self.lastw:
            self.sig = {k: v for k, v in self.sig.items() if k in set(self.last_on.values()) | set(self.dma_last.values())}


class WRing:
    def __init__(self, sch, nc, name, tens, nslot):
        self.sch, self.nc, self.name, self.tens, self.nslot = sch, nc, name, tens, nslot
        self.pieces = []
        self.nl = 0
        self.nu = 0

    def start(self, pieces):
        self.pieces = pieces
        self.nl = 0
        self.nu = 0
        for _ in range(self.nslot):
            self._load()

    def _load(self):
        i = self.nl
        if i >= len(self.pieces):
            return
        slot = i % self.nslot
        ap, n = self.pieces[i]
        tens, nc = self.tens, self.nc
        self.sch.op("sp", lambda: nc.sync.dma_start(out=tens[:, slot, 0:n], in_=ap),
                    w=[(self.name, slot)], dkey=(self.name, slot))
        self.nl += 1

    def get(self):
        s = self.nu % self.nslot
        self.nu += 1
        return s

    def done(self):
        self._load()


def build(NT, dbg=()):
    NTILE = NT // T
    NKT = NT // 128
    nc = bass.Bass("TRN2", target_bir_lowering=False)
    es = ExitStack()

    def din(name, shape, dt=F32):
        return nc.dram_tensor(name, shape, dt, kind="ExternalInput").ap()

    x_in = din("x", [NT, D])
    pos_in = din("pos", [1, NT])
    amask_in = din("amask", [1, NKT * NTILE])
    keep_in = din("keep", [1, 1])
    w_in = din("w_in", [2, D, 4096])
    w_out = din("w_out", [2, D, D])
    attn_lambda = din("attn_lambda", [2, 4, 64])
    attn_norm_g = din("attn_norm_g", [2, 128])
    hg_norm_g = din("hg_norm_g", [2, 128])
    hg_lb = din("hg_lower_bound", [2, 2, 512])
    w_gate = din("ffn_w_gate", [2, 2, D, DFF])
    w_up = din("ffn_w_up", [2, 2, D, DFF])
    w_down = din("ffn_w_down", [2, 2, DFF, D])
    ln_g = din("ln_g", [2, 3, D])
    ln_b = din("ln_b", [2, 3, D])
    y_out = nc.dram_tensor("y", [NT, D], F32, kind="ExternalOutput").ap()

    def scr(name, shape, dt):
        kind = "ExternalOutput" if name in dbg else "Internal"
        return nc.dram_tensor(name, shape, dt, kind=kind).ap()

    WGU = scr("WGU", [2, 2, 2, FC, 128, KC * 128], BF16)
    WD = scr("WD", [2, 2, 8, 128, FC * 128], BF16)
    WIN = scr("WIN", [2, 32, 128, KC * 128], BF16)
    WOUT = scr("WOUT", [2, 8, 128, KC * 128], BF16)
    WINS = scr("WINS", [2, 8, 128, KC * 128], BF16)
    CT = scr("CT", [128, NT], F32)
    ST = scr("ST", [128, NT], F32)
    X1T = scr("X1T", [8, 128, NT], F32)
    QKT = scr("QKT", [8, 128, NT], BF16)
    VS = scr("VS", [4, NKT, 128, 129], BF16)
    HQ = scr("HQ", [4, 128, NT], BF16)
    ZF = scr("ZF", [4, 128, NT], F32)
    ZB = scr("ZB", [4, 128, NT], F32)
    GATE = scr("GATE", [4, 128, NT], BF16)
    HI = scr("HI", [NT, 512], BF16)
    OFWD = scr("OFWD", [4, 128, NT], F32)
    MIXT = scr("MIXT", [8, 128, NT], BF16)

    sch = Sch(nc, es)
    op = sch.op

    uniq = [0]

    def sb(stack, name, shape, dt):
        uniq[0] += 1
        return stack.enter_context(nc.sbuf_tensor("%s_%d" % (name, uniq[0]), shape, dt))

    def pst(stack, name, shape, dt):
        uniq[0] += 1
        return stack.enter_context(nc.psum_tensor("%s_%d" % (name, uniq[0]), shape, dt))

    ident_bf = sb(es, "ident_bf", [128, 128], BF16)
    ident_f = sb(es, "ident_f", [128, 128], F32)
    onesm = sb(es, "onesm", [128, 128], F32)
    ones128 = sb(es, "ones128", [128, 128], F32)
    blk = sb(es, "blk", [128, 2, 128], BF16)
    maskf = sb(es, "maskf", [64, 64], F32)
    maskb = sb(es, "maskb", [64, 64], F32)
    rmask = sb(es, "rmask", [128, T], F32)
    lng = sb(es, "lng", [128, 48], F32)
    lnb = sb(es, "lnb", [128, 48], F32)
    hgg = sb(es, "hgg", [128, 2], F32)
    agb = sb(es, "agb", [128, 256], F32)
    lam = sb(es, "lam", [128, 2], F32)
    lbt = sb(es, "lbt", [128, 16], F32)
    l1m = sb(es, "l1m", [128, 16], F32)
    amask = sb(es, "amask_sb", [128, NKT * NTILE], F32)
    keep = sb(es, "keep_sb", [128, 1], F32)
    qkmax = sb(es, "qkmax", [128, 32], F32)
    mhalf = sb(es, "mhalf", [128, 4], F32)
    negM = sb(es, "negM", [128, 16], F32)

    with ExitStack() as ps:
        g = nc.gpsimd
        v = nc.vector
        a = nc.scalar

        def memset(t_, val, tok):
            op("pool", lambda: g.memset(t_, val), w=[tok])

        def affsel(t_, pattern, cmp, base, cm, tok):
            op("pool", lambda: g.affine_select(out=t_, in_=t_, pattern=pattern, compare_op=cmp, fill=0.0,
                                               base=base, channel_multiplier=cm), r=[tok], w=[tok])

        memset(ident_bf[:, :], 1.0, "ident_bf")
        affsel(ident_bf[:, :], [[-1, 128]], ALU.is_equal, 0, 1, "ident_bf")
        memset(ident_f[:, :], 1.0, "ident_f")
        affsel(ident_f[:, :], [[-1, 128]], ALU.is_equal, 0, 1, "ident_f")
        memset(onesm[:, :], 1.0 / 1024.0, "c1")
        memset(ones128[:, :], 1.0 / 128.0, "c2")
        memset(blk[:, :, :], 0.0, "blk")
        memset(blk[0:64, 0, :], 1.0, "blk")
        memset(blk[64:128, 1, :], 1.0, "blk")
        memset(maskf[:, :], 1.0, "maskf")
        affsel(maskf[:, :], [[1, 64]], ALU.is_ge, 0, -1, "maskf")
        memset(maskb[:, :], 1.0, "maskb")
        affsel(maskb[:, :], [[-1, 64]], ALU.is_ge, 0, 1, "maskb")
        memset(rmask[:, :], 1.0, "rmask")
        memset(rmask[:, ::64], 0.0, "rmask")
        memset(mhalf[:, :], -0.5, "mhalf")
        memset(qkmax[:, :], 0.0, "qkmax")

        def ld(t_, src, tok, key):
            op("sp", lambda: nc.sync.dma_start(out=t_, in_=src, allow_slow_non_contiguous=True), w=[tok], dkey=key)

        ld(lng[:, :].rearrange("p (a k) -> p a k", k=8), ln_g.rearrange("l i (k p) -> p (l i) k", p=128), "lng", "p0")
        ld(lnb[:, :].rearrange("p (a k) -> p a k", k=8), ln_b.rearrange("l i (k p) -> p (l i) k", p=128), "lnb", "p1")
        ld(hgg[:, :], hg_norm_g.rearrange("l p -> p l"), "hgg", "p2")
        ld(amask[:, :], amask_in.partition_broadcast(128), "amask", "p3")
        ld(keep[:, :], keep_in.partition_broadcast(128), "keep", "p4")
        agt = sb(ps, "agt", [128, 256], F32)
        ld(agt[:, :], attn_norm_g.rearrange("l c -> (l c)").partition_broadcast(128), "agt", "p5")
        for l in range(2):
            op("dve", lambda l=l: v.tensor_scalar(out=agb[:, l * 128:(l + 1) * 128], in0=agt[:, l * 128:(l + 1) * 128],
                                                 scalar1=1.0 - LAM_INIT[l], scalar2=None, op0=ALU.mult),
               r=["agt"], w=[("agb", l)])
        lpt = sb(ps, "lpt", [128, 512], F32)
        ld(lpt[:, :], attn_lambda.rearrange("l a c -> (l a c)").partition_broadcast(128), "lpt", "p6")
        lpp = sb(ps, "lpp", [128, 4, 64], F32)
        lps = sb(ps, "lps", [128, 4], F32)
        lpe = sb(ps, "lpe", [128, 4], F32)
        lp4 = lpt[:, :].rearrange("p (l a c) -> p l a c", l=2, a=4)
        for l in range(2):
            for k in range(2):
                op("dve", lambda l=l, k=k: v.tensor_tensor(out=lpp[:, l * 2 + k, :], in0=lp4[:, l, 2 * k, :],
                                                          in1=lp4[:, l, 2 * k + 1, :], op=ALU.mult),
                   r=["lpt"], w=[("lpp", l, k)])
        op("dve", lambda: v.tensor_reduce(out=lps[:, :], in_=lpp[:, :, :], axis=AX.X, op=ALU.add),
           r=[("lpp", l, k) for l in range(2) for k in range(2)], w=["lps"])
        op("act", lambda: a.activation(out=lpe[:, :], in_=lps[:, :], func=AF.Exp), r=["lps"], w=["lpe"])
        for l in range(2):
            op("dve", lambda l=l: v.tensor_tensor(out=lam[:, l:l + 1], in0=lpe[:, 2 * l:2 * l + 1],
                                                 in1=lpe[:, 2 * l + 1:2 * l + 2], op=ALU.subtract),
               r=["lpe"], w=[("lam", l)])
            op("dve", lambda l=l: v.tensor_scalar(out=lam[:, l:l + 1], in0=lam[:, l:l + 1], scalar1=LAM_INIT[l],
                                                 scalar2=None, op0=ALU.add), r=[("lam", l)], w=[("lam", l)])
        lbi = sb(ps, "lbi", [128, 16], F32)
        ld(lbi[:, :].rearrange("p (a h) -> p a h", h=4), hg_lb.rearrange("d l (h p) -> p (d l) h", p=128), "lbi", "p7")
        lbm = sb(ps, "lbm", [128, 2, 4], F32)
        lbe = sb(ps, "lbe", [128, 16], F32)
        lbs_ = sb(ps, "lbs_", [128, 2, 4], F32)
        lbi4 = lbi[:, :].rearrange("p (d l h) -> p d l h", d=2, l=2)
        lbe4 = lbe[:, :].rearrange("p (d l h) -> p d l h", d=2, l=2)
        op("dve", lambda: v.tensor_tensor(out=lbm[:, :, :], in0=lbi4[:, :, 0, :], in1=lbi4[:, :, 1, :], op=ALU.max),
           r=["lbi"], w=["lbm"])
        for l in range(2):
            op("dve", lambda l=l: v.tensor_tensor(out=lbe4[:, :, l, :], in0=lbi4[:, :, l, :], in1=lbm[:, :, :],
                                                 op=ALU.subtract), r=["lbi", "lbm"], w=[("lbe", l)])
        op("act", lambda: a.activation(out=lbe[:, :], in_=lbe[:, :], func=AF.Exp),
           r=[("lbe", 0), ("lbe", 1)], w=[("lbe", 0), ("lbe", 1)])
        op("dve", lambda: v.tensor_tensor(out=lbs_[:, :, :], in0=lbe4[:, :, 0, :], in1=lbe4[:, :, 1, :], op=ALU.add),
           r=[("lbe", 0), ("lbe", 1)], w=["lbs_"])
        op("dve", lambda: v.reciprocal(out=lbs_[:, :, :], in_=lbs_[:, :, :]), r=["lbs_"], w=["lbs_"])
        for l in range(2):
            op("dve", lambda l=l: v.tensor_tensor(out=lbe4[:, :, l, :], in0=lbe4[:, :, l, :], in1=lbs_[:, :, :],
                                                 op=ALU.mult), r=[("lbe", l), "lbs_"], w=[("lbe", l)])
        lbc = sb(ps, "lbc", [128, 2, 4], F32)
        lbt4 = lbt[:, :].rearrange("p (l d h) -> p l d h", l=2, d=2)
        op("dve", lambda: v.tensor_tensor(out=lbt4[:, 0, :, :], in0=lbe4[:, :, 0, :], in1=lbe4[:, :, 0, :],
                                          op=ALU.subtract), r=[("lbe", 0)], w=[("lbt", 0)])
        op("dve", lambda: v.tensor_tensor(out=lbc[:, :, :], in0=lbe4[:, :, 0, :], in1=lbe4[:, :, 1, :], op=ALU.add),
           r=[("lbe", 0), ("lbe", 1)], w=["lbc"])
        op("dve", lambda: v.tensor_tensor(out=lbt4[:, 1, :, :], in0=lbc[:, :, :], in1=lbe4[:, :, 0, :],
                                          op=ALU.subtract), r=["lbc", ("lbe", 0)], w=[("lbt", 1)])
        op("dve", lambda: v.tensor_scalar(out=lbt[:, :], in0=lbt[:, :], scalar1=0.0, scalar2=None, op0=ALU.max),
           r=[("lbt", 0), ("lbt", 1)], w=["lbt"])
        op("act", lambda: a.activation(out=l1m[:, :], in_=lbt[:, :], func=AF.Ln, scale=-1.0, bias=1.0),
           r=["lbt"], w=["l1m"])

        pidx = sb(ps, "pidx", [128, 1], I32)
        pjf = sb(ps, "pjf", [128, 4], F32)
        invf = sb(ps, "invf", [128, 1], F32)
        rotm = sb(ps, "rotm", [128, 1], F32)
        sgn = sb(ps, "sgn", [128, 1], F32)
        tmp1 = sb(ps, "tmp1", [128, 1], F32)
        op("pool", lambda: g.iota(pidx[:, :], pattern=[[0, 1]], base=0, channel_multiplier=1), w=["pidx"])
        op("dve", lambda: v.tensor_copy(out=pjf[:, 1:2], in_=pidx[:, :]), r=["pidx"], w=["pjf"])
        op("dve", lambda: v.tensor_scalar(out=pjf[:, 3:4], in0=pjf[:, 1:2], scalar1=64.0, scalar2=-64.0, op0=ALU.is_ge,
                                          op1=ALU.mult), r=["pjf"], w=["pjf3"])
        op("dve", lambda: v.tensor_tensor(out=pjf[:, 1:2], in0=pjf[:, 1:2], in1=pjf[:, 3:4], op=ALU.add),
           r=["pjf", "pjf3"], w=["pjf"])
        op("dve", lambda: v.tensor_copy(out=pjf[:, 2:3], in_=pjf[:, 1:2]), r=["pjf"], w=["pjf2"])
        for m in (16.0, 32.0, 48.0):
            op("dve", lambda m=m: v.tensor_scalar(out=pjf[:, 3:4], in0=pjf[:, 1:2], scalar1=m, scalar2=-16.0, op0=ALU.is_ge,
                                                  op1=ALU.mult), r=["pjf"], w=["pjf3"])
            op("dve", lambda: v.tensor_tensor(out=pjf[:, 2:3], in0=pjf[:, 2:3], in1=pjf[:, 3:4], op=ALU.add),
               r=["pjf2", "pjf3"], w=["pjf2"])
        op("dve", lambda: v.tensor_scalar(out=pjf[:, 3:4], in0=pjf[:, 2:3], scalar1=8.0, scalar2=-8.0, op0=ALU.is_ge,
                                          op1=ALU.mult), r=["pjf2"], w=["pjf3"])
        op("dve", lambda: v.tensor_tensor(out=pjf[:, 0:1], in0=pjf[:, 2:3], in1=pjf[:, 3:4], op=ALU.add),
           r=["pjf2", "pjf3"], w=["pjf0"])
        op("dve", lambda: v.memset(invf[:, :], 0.0), w=["invf"])
        for i in range(8):
            ci = float(np.float32(THETA) ** np.float32(-i * 2.0 / 16.0))
            op("dve", lambda i=i, ci=ci: v.tensor_scalar(out=tmp1[:, :], in0=pjf[:, 0:1], scalar1=float(i), scalar2=ci,
                                                        op0=ALU.is_equal, op1=ALU.mult), r=["pjf0"], w=["tmp1"])
            op("dve", lambda: v.tensor_tensor(out=invf[:, :], in0=invf[:, :], in1=tmp1[:, :], op=ALU.add),
               r=["tmp1", "invf"], w=["invf"])
        op("dve", lambda: v.tensor_single_scalar(out=rotm[:, :], in_=pjf[:, 1:2], scalar=16.0, op=ALU.is_lt),
           r=["pjf"], w=["rotm"])
        op("dve", lambda: v.tensor_scalar(out=sgn[:, :], in0=pjf[:, 2:3], scalar1=8.0, scalar2=None, op0=ALU.is_ge),
           r=["pjf2"], w=["sgn"])
        op("dve", lambda: v.tensor_scalar(out=sgn[:, :], in0=sgn[:, :], scalar1=2.0, scalar2=-1.0, op0=ALU.mult,
                                          op1=ALU.add), r=["sgn"], w=["sgn"])
        op("dve", lambda: v.tensor_tensor(out=sgn[:, :], in0=sgn[:, :], in1=rotm[:, :], op=ALU.mult),
           r=["sgn", "rotm"], w=["sgn"])
        onem = sb(ps, "onem", [128, 1], F32)
        op("dve", lambda: v.tensor_scalar(out=onem[:, :], in0=rotm[:, :], scalar1=-1.0, scalar2=1.0, op0=ALU.mult,
                                          op1=ALU.add), r=["rotm"], w=["onem"])
        PW = min(NT, 2048)
        posb = sb(ps, "posb", [128, PW], F32)
        ua = sb(ps, "ua", [128, PW], F32)
        ub = sb(ps, "ub", [128, PW], F32)
        ui = sb(ps, "ui", [128, PW], I32)
        uc = sb(ps, "uc", [128, PW], F32)
        cs = sb(ps, "cs", [128, 2, PW], F32)
        for pc in range(NT // PW):
            sl = slice(pc * PW, (pc + 1) * PW)
            ld(posb[:, :], pos_in[:, sl].partition_broadcast(128), "posb", "p8")
            op("dve", lambda: v.tensor_scalar(out=ua[:, :], in0=posb[:, :], scalar1=invf[:, 0:1],
                                              scalar2=float(1.0 / (2.0 * math.pi)), op0=ALU.mult, op1=ALU.mult),
               r=["posb", "invf"], w=["ua"])
            for which in range(2):
                if which == 0:
                    op("dve", lambda: v.tensor_scalar(out=ub[:, :], in0=ua[:, :], scalar1=0.25, scalar2=None,
                                                      op0=ALU.add), r=["ua"], w=["ub"])
                else:
                    op("dve", lambda: v.tensor_copy(out=ub[:, :], in_=ua[:, :]), r=["ua"], w=["ub"])
                op("dve", lambda: v.tensor_copy(out=ui[:, :], in_=ub[:, :]), r=["ub"], w=["ui"])
                op("dve", lambda: v.tensor_copy(out=uc[:, :], in_=ui[:, :]), r=["ui"], w=["uc"])
                op("dve", lambda: v.tensor_tensor(out=ub[:, :], in0=ub[:, :], in1=uc[:, :], op=ALU.subtract),
                   r=["ub", "uc"], w=["ub"])
                op("dve", lambda: v.tensor_single_scalar(out=uc[:, :], in_=ub[:, :], scalar=0.5, op=ALU.is_gt),
                   r=["ub"], w=["uc"])
                op("dve", lambda: v.tensor_tensor(out=ub[:, :], in0=ub[:, :], in1=uc[:, :], op=ALU.subtract),
                   r=["ub", "uc"], w=["ub"])
                op("dve", lambda: v.tensor_single_scalar(out=uc[:, :], in_=ub[:, :], scalar=-0.5, op=ALU.is_lt),
                   r=["ub"], w=["uc"])
                op("dve", lambda: v.tensor_tensor(out=ub[:, :], in0=ub[:, :], in1=uc[:, :], op=ALU.add),
                   r=["ub", "uc"], w=["ub"])
                op("act", lambda which=which: a.activation(out=cs[:, which, :], in_=ub[:, :], func=AF.Sin,
                                                           scale=float(2.0 * math.pi)),
                   r=["ub"], w=[("cs", which)])
            op("dve", lambda: v.tensor_scalar(out=cs[:, 0, :], in0=cs[:, 0, :], scalar1=rotm[:, 0:1],
                                              scalar2=onem[:, 0:1], op0=ALU.mult, op1=ALU.add),
               r=[("cs", 0), "rotm", "onem"], w=[("cs", 0)])
            op("dve", lambda: v.tensor_scalar(out=cs[:, 1, :], in0=cs[:, 1, :], scalar1=sgn[:, 0:1], scalar2=None,
                                              op0=ALU.mult), r=[("cs", 1), "sgn"], w=[("cs", 1)])
            op("sp", lambda sl=sl: nc.sync.dma_start(out=CT[:, sl], in_=cs[:, 0, :]), r=[("cs", 0)], dkey="p9")
            op("sp", lambda sl=sl: nc.sync.dma_start(out=ST[:, sl], in_=cs[:, 1, :]), r=[("cs", 1)], dkey="p10")
        sch.barrier()

    with ExitStack() as ps:
        stage = sb(ps, "wstage", [128, 2, 11264], F32)
        pct = sb(ps, "wpct", [128, 6, FC * 128], BF16)
        pcs = sb(ps, "wpcs", [128, 2, KC * 128], BF16)
        jobs = []
        for l in range(2):
            for i in range(2):
                for gu, wsrc in enumerate((w_gate, w_up)):
                    for half in range(2):
                        jobs.append((wsrc[l, i], KC, half * 1408, 1408,
                                     lambda j, l=l, i=i, gu=gu, half=half: WGU[l, i, gu, half * 11 + j]))
                for blk_ in range(4):
                    jobs.append((w_down[l, i], FC, blk_ * 256, 256,
                                 lambda j, l=l, i=i, blk_=blk_: WD[l, i, blk_ * 2 + j]))
            for blk_ in range(4):
                jobs.append((w_in[l], KC, blk_ * 1024, 1024, lambda j, l=l, blk_=blk_: WIN[l, blk_ * 8 + j],
                             (lambda j, l=l: WINS[l, j]) if blk_ == 0 else None))
            jobs.append((w_out[l], KC, 0, 1024, lambda j, l=l: WOUT[l, j]))
        cnt = 0
        engs3 = ("dve", "pool", "act")
        swc = 0
        for jb, job in enumerate(jobs):
            src, kcw, c0, ncol, dst = job[:5]
            dsts = job[5] if len(job) > 5 else None
            sslot = jb % 2
            sview = stage[:, sslot, 0:kcw * ncol].rearrange("p (k c) -> p k c", k=kcw)
            op("sp", lambda src=src, c0=c0, ncol=ncol, sview=sview: nc.sync.dma_start(
                out=sview, in_=src[:, c0:c0 + ncol].rearrange("(k p) c -> p k c", p=128)),
               w=[("stage", sslot)], dkey=("stage", sslot))
            for j in range(ncol // 128):
                pslot = cnt % 6
                e = engs3[cnt % 3]
                cnt += 1
                outv = pct[:, pslot, 0:kcw * 128].rearrange("p (k c) -> p k c", k=kcw)
                inv = sview[:, :, j * 128:(j + 1) * 128]
                if e == "act":
                    op("act", lambda outv=outv, inv=inv: nc.scalar.copy(out=outv, in_=inv),
                       r=[("stage", sslot)], w=[("pct", pslot)])
                elif e == "dve":
                    op("dve", lambda outv=outv, inv=inv: nc.vector.tensor_copy(out=outv, in_=inv),
                       r=[("stage", sslot)], w=[("pct", pslot)])
                else:
                    op("pool", lambda outv=outv, inv=inv: nc.gpsimd.tensor_copy(out=outv, in_=inv),
                       r=[("stage", sslot)], w=[("pct", pslot)])
                dap = dst(j)
                op("sp", lambda dap=dap, pslot=pslot, kcw=kcw: nc.sync.dma_start(out=dap, in_=pct[:, pslot, 0:kcw * 128]),
                   r=[("pct", pslot)], dkey=("pct", pslot))
                if dsts is not None:
                    ss = swc % 2
                    swc += 1
                    w4 = pct[:, pslot, 0:KC * 128].rearrange("p (k c d) -> p k c d", k=KC, c=2)
                    s4 = pcs[:, ss, :].rearrange("p (k c d) -> p k c d", k=KC, c=2)
                    op("dve", lambda w4=w4, s4=s4: nc.vector.tensor_copy(out=s4[:, :, :, 0:8], in_=w4[:, :, :, 8:16]),
                       r=[("pct", pslot)], w=[("pcs", ss, 0)])
                    op("dve", lambda w4=w4, s4=s4: nc.vector.tensor_copy(out=s4[:, :, :, 8:16], in_=w4[:, :, :, 0:8]),
                       r=[("pct", pslot)], w=[("pcs", ss, 1)])
                    op("dve", lambda w4=w4, s4=s4: nc.vector.tensor_copy(out=s4[:, :, :, 16:64], in_=w4[:, :, :, 16:64]),
                       r=[("pct", pslot)], w=[("pcs", ss, 2)])
                    dap2 = dsts(j)
                    op("sp", lambda dap2=dap2, ss=ss: nc.sync.dma_start(out=dap2, in_=pcs[:, ss, :]),
                       r=[("pcs", ss, 0), ("pcs", ss, 1), ("pcs", ss, 2)], dkey=("pcs", ss))
        sch.barrier()

    cut = [x_ for x_ in dbg if isinstance(x_, str) and x_.startswith("cut:")]
    cut = int(cut[0][4:]) if cut else 99

    def chain_phase(do_D, do_A, first, last):
        with ExitStack() as cs_:
            xT = sb(cs_, "xT", [128, KC, T], F32)
            xbf = sb(cs_, "xbf", [128, KC, T], BF16)
            hT = sb(cs_, "hT", [128, FC, T], BF16)
            sq = sb(cs_, "sq", [128, KC, T], F32)
            zsum = sb(cs_, "zsum", [128, T], F32)
            ssum = sb(cs_, "ssum", [128, T], F32)
            mean_sb = sb(cs_, "mean_sb", [128, T], F32)
            msq = sb(cs_, "msq", [128, T], F32)
            rstd = sb(cs_, "rstd", [128, T], F32)
            sgt = sb(cs_, "sgt", [128, 2, T], F32)
            mixbf = sb(cs_, "mixbf", [128, KC, T], BF16) if do_D is not None else None
            stb = sb(cs_, "stb", [128, 4, T], BF16)
            stf = sb(cs_, "stf", [128, 4, T], F32)
            rA = sb(cs_, "rA", [128, 2, T], F32)
            vst = sb(cs_, "vst", [128, 4, 4, 129], BF16)
            hist = sb(cs_, "hist", [128, 4, 512], BF16)
            ctt = sb(cs_, "ctt", [128, 2, T], F32)
            tok = sb(cs_, "tok", [128, 4, D], F32) if (first or last) else None
            ringA_t = sb(cs_, "ringA", [128, 12, KC * 128], BF16)
            ringB_t = sb(cs_, "ringB", [128, 3, FC * 128], BF16)
            mx = sb(cs_, "mx", [128, 4], F32)
            sqb = sb(cs_, "sqb", [128, 2, T], BF16)
            pss = [pst(cs_, "cps%d" % i, [128, T], F32) for i in range(8)]
            ringA = WRing(sch, nc, "rA", ringA_t, 12)
            ringB = WRing(sch, nc, "rB", ringB_t, 3)
            psn = [0]
            stn = [0, 0]

            def newps():
                i = psn[0] % 8
                psn[0] += 1
                return i

            pa, pb = [], []
            for t in range(NTILE):
                if do_D is not None:
                    l = do_D
                    for j in range(8):
                        pa.append((WOUT[l, j], KC * 128))
                    for j in range(FC):
                        pa.append((WGU[l, 1, 0, j], KC * 128))
                        pa.append((WGU[l, 1, 1, j], KC * 128))
                    for j in range(8):
                        pb.append((WD[l, 1, j], FC * 128))
                if do_A is not None:
                    l = do_A
                    for j in range(FC):
                        pa.append((WGU[l, 0, 0, j], KC * 128))
                        pa.append((WGU[l, 0, 1, j], KC * 128))
                    for j in range(8):
                        pb.append((WD[l, 0, j], FC * 128))
                    for j in range(32):
                        pa.append((WIN[l, j], KC * 128))
                        if j < 8:
                            pa.append((WINS[l, j], KC * 128))
            if "no:ring" in dbg:
                pa, pb = [], []
            ringA.start(pa)
            ringB.start(pb)
            if do_A is not None and "no:vst" not in dbg:
                op("pool", lambda: nc.gpsimd.memset(vst[:, :, :, :], 1.0), w=["vst"])

            def mm_fm(ring, ringt, kcn, rhs_of, rtoks, pi, ptok_extra=()):
                s = ring.get()
                for kc in range(kcn):
                    op("pe", lambda s=s, kc=kc: nc.tensor.matmul(pss[pi][:, :], lhsT=ringt[:, s, kc * 128:(kc + 1) * 128],
                                                               rhs=rhs_of(kc), start=(kc == 0), stop=(kc == kcn - 1)),
                       r=[(ring.name, s)] + ([rtoks[kc]] if len(rtoks) == kcn else rtoks), w=[("ps", pi)])
                ring.done()

            def stats_acc(dc):
                op("act", lambda dc=dc: nc.scalar.activation(out=sq[:, dc, :], in_=xT[:, dc, :], func=AF.Square),
                   r=[("xT", dc)], w=[("sq", dc)])
                if dc == 0:
                    op("dve", lambda: nc.vector.tensor_copy(out=zsum[:, :], in_=xT[:, 0, :]), r=[("xT", 0)], w=["zsum"])
                    op("dve", lambda: nc.vector.tensor_copy(out=ssum[:, :], in_=sq[:, 0, :]), r=[("sq", 0)], w=["ssum"])
                else:
                    op("dve", lambda dc=dc: nc.vector.tensor_tensor(out=zsum[:, :], in0=zsum[:, :], in1=xT[:, dc, :], op=ALU.add),
                       r=[("xT", dc), "zsum"], w=["zsum"])
                    op("dve", lambda dc=dc: nc.vector.tensor_tensor(out=ssum[:, :], in0=ssum[:, :], in1=sq[:, dc, :], op=ALU.add),
                       r=[("sq", dc), "ssum"], w=["ssum"])

            def layer_norm(l, i):
                gi = (l * 3 + i) * 8
                eps_eff = EPS / (ALPHA * ALPHA)
                pm = newps()
                op("pe", lambda: nc.tensor.matmul(pss[pm][:, :], lhsT=onesm[:, :], rhs=zsum[:, :], start=True, stop=True),
                   r=["zsum"], w=[("ps", pm)])
                pv = newps()
                op("pe", lambda: nc.tensor.matmul(pss[pv][:, :], lhsT=onesm[:, :], rhs=ssum[:, :], start=True, stop=True),
                   r=["ssum"], w=[("ps", pv)])
                op("act", lambda: nc.scalar.activation(out=mean_sb[:, :], in_=pss[pm][:, :], func=AF.Identity),
                   r=[("ps", pm)], w=["mean_sb"])
                op("act", lambda: nc.scalar.activation(out=msq[:, :], in_=mean_sb[:, :], func=AF.Square),
                   r=["mean_sb"], w=["msq"])
                op("dve", lambda: nc.vector.tensor_tensor(out=rstd[:, :], in0=pss[pv][:, :], in1=msq[:, :], op=ALU.subtract),
                   r=[("ps", pv), "msq"], w=["rstd"])
                op("dve", lambda: nc.vector.tensor_scalar(out=rstd[:, :], in0=rstd[:, :], scalar1=0.0, scalar2=None, op0=ALU.max),
                   r=["rstd"], w=["rstd"])
                op("act", lambda: nc.scalar.activation(out=rstd[:, :], in_=rstd[:, :], func=AF.Sqrt, bias=eps_eff, scale=1.0),
                   r=["rstd"], w=["rstd"])
                op("dve", lambda: nc.vector.reciprocal(out=rstd[:, :], in_=rstd[:, :]), r=["rstd"], w=["rstd"])
                for k in range(KC):
                    op("dve", lambda k=k: nc.vector.tensor_tensor(out=xT[:, k, :], in0=xT[:, k, :], in1=mean_sb[:, :], op=ALU.subtract),
                       r=[("xT", k), "mean_sb"], w=[("xT", k)])
                    op("dve", lambda k=k: nc.vector.tensor_tensor(out=xT[:, k, :], in0=xT[:, k, :], in1=rstd[:, :], op=ALU.mult),
                       r=[("xT", k), "rstd"], w=[("xT", k)])
                    op("act", lambda k=k: nc.scalar.activation(out=xT[:, k, :], in_=xT[:, k, :], func=AF.Identity,
                                                               scale=lng[:, gi + k:gi + k + 1], bias=lnb[:, gi + k:gi + k + 1]),
                       r=[("xT", k)], w=[("xT", k)])
                    op("dve", lambda k=k: nc.vector.tensor_copy(out=xbf[:, k, :], in_=xT[:, k, :]),
                       r=[("xT", k)], w=[("xbf", k)])

            def ffn(l, i):
                xb_all = [("xbf", k) for k in range(KC)]
                if cut < 2:
                    return
                for j in range(FC):
                    pg, pu = newps(), newps()
                    mm_fm(ringA, ringA_t, KC, lambda kc: xbf[:, kc, :], xb_all, pg)
                    mm_fm(ringA, ringA_t, KC, lambda kc: xbf[:, kc, :], xb_all, pu)
                    b = j % 2
                    op("act", lambda pg=pg, b=b: nc.scalar.activation(out=sgt[:, b, :], in_=pss[pg][:, :], func=AF.Silu),
                       r=[("ps", pg)], w=[("sgt", b)])
                    op("dve", lambda pu=pu, b=b, j=j: nc.vector.tensor_tensor(out=hT[:, j, :], in0=sgt[:, b, :],
                                                                           in1=pss[pu][:, :], op=ALU.mult),
                       r=[("sgt", b), ("ps", pu)], w=[("hT", j)])
                h_all = [("hT", j) for j in range(FC)]
                if cut < 3:
                    return
                for dc in range(8):
                    pd = newps()
                    mm_fm(ringB, ringB_t, FC, lambda kc: hT[:, kc, :], h_all, pd)
                    op("dve", lambda pd=pd, dc=dc: nc.vector.scalar_tensor_tensor(
                        out=xT[:, dc, :], in0=pss[pd][:, :], scalar=float(0.5 / ALPHA), in1=xT[:, dc, :],
                        op0=ALU.mult, op1=ALU.add), r=[("ps", pd), ("xT", dc)], w=[("xT", dc)])
                    stats_acc(dc)
                if cut < 4:
                    return
                layer_norm(l, 2 * i)

            def stage_b():
                i = stn[0] % 4
                stn[0] += 1
                return i

            def stage_f():
                i = stn[1] % 4
                stn[1] += 1
                return i

            for t in range(NTILE):
                tsl = slice(t * T, (t + 1) * T)
                if first and "no:first" not in dbg:
                    op("sp", lambda t=t: nc.sync.dma_start(out=tok[:, :, :],
                                                           in_=x_in[t * T:(t + 1) * T, :].rearrange("(s p) d -> p s d", p=128)),
                       w=["tok"], dkey="tok")
                    for k in range(KC):
                        pi = newps()
                        for s in range(4):
                            op("pe", lambda k=k, s=s, pi=pi: nc.tensor.transpose(pss[pi][:, s * 128:(s + 1) * 128],
                                                                               tok[:, s, k * 128:(k + 1) * 128], ident_f[:, :]),
                               r=["tok"], w=[("ps", pi)])
                        op("dve", lambda k=k, pi=pi: nc.vector.tensor_copy(out=xT[:, k, :], in_=pss[pi][:, :]),
                           r=[("ps", pi)], w=[("xT", k)])
                        op("act", lambda k=k: nc.scalar.activation(out=xbf[:, k, :], in_=xT[:, k, :], func=AF.Identity),
                           r=[("xT", k)], w=[("xbf", k)])
                if do_D is not None:
                    l = do_D
                    op("sp", lambda tsl=tsl: nc.sync.dma_start(out=mixbf[:, :, :], in_=MIXT[:, :, tsl].rearrange("k p t -> p k t")),
                       w=["mixbf"], dkey="mixbf")
                    op("sp", lambda tsl=tsl: nc.sync.dma_start(out=xT[:, :, :], in_=X1T[:, :, tsl].rearrange("k p t -> p k t")),
                       w=[("xT", k) for k in range(KC)], dkey="x1ld")
                    for dc in range(8):
                        pd = newps()
                        mm_fm(ringA, ringA_t, KC, lambda kc: mixbf[:, kc, :], ["mixbf"], pd)
                        op("dve", lambda pd=pd, dc=dc: nc.vector.scalar_tensor_tensor(
                            out=xT[:, dc, :], in0=pss[pd][:, :], scalar=float(1.0 / ALPHA), in1=xT[:, dc, :],
                            op0=ALU.mult, op1=ALU.add), r=[("ps", pd), ("xT", dc)], w=[("xT", dc)])
                        stats_acc(dc)
                    layer_norm(l, 1)
                    ffn(l, 1)
                if last:
                    for s in range(4):
                        for half in range(2):
                            pi = newps()
                            for kk in range(4):
                                k = half * 4 + kk
                                op("pe", lambda k=k, kk=kk, s=s, pi=pi: nc.tensor.transpose(
                                    pss[pi][:, kk * 128:(kk + 1) * 128], xT[:, k, s * 128:(s + 1) * 128], ident_f[:, :]),
                                   r=[("xT", k)], w=[("ps", pi)])
                            op("dve", lambda s=s, half=half, pi=pi: nc.vector.tensor_copy(
                                out=tok[:, s, half * 512:(half + 1) * 512], in_=pss[pi][:, :]),
                               r=[("ps", pi)], w=[("tok", s, half)])
                    op("sp", lambda t=t: nc.sync.dma_start(out=y_out[t * T:(t + 1) * T, :].rearrange("(s p) d -> p s d", p=128),
                                                           in_=tok[:, :, :]),
                       r=[("tok", s, h_) for s in range(4) for h_ in range(2)], dkey="ystore")
                if do_A is not None:
                    l = do_A
                    ffn(l, 0)
                    if cut < 5:
                        break
                    op("sp", lambda tsl=tsl: nc.sync.dma_start(out=X1T[:, :, tsl].rearrange("k p t -> p k t"), in_=xT[:, :, :]),
                       r=[("xT", k) for k in range(KC)], dkey="x1st")
                    op("sp", lambda tsl=tsl: nc.sync.dma_start(out=ctt[:, 0, :], in_=CT[:, tsl]), w=[("ctt", 0)], dkey="ctl")
                    op("sp", lambda tsl=tsl: nc.sync.dma_start(out=ctt[:, 1, :], in_=ST[:, tsl]), w=[("ctt", 1)], dkey="stl")
                    xb_all = [("xbf", k) for k in range(KC)]
                    for e in range(32):
                        if cut < 6 + e:
                            break
                        if e < 8:
                            h = e % 4
                            qk = e // 4
                            pA, pB = newps(), newps()
                            s = ringA.get()
                            for kc in range(KC):
                                op("pe", lambda s=s, kc=kc, pA=pA: nc.tensor.matmul(
                                    pss[pA][:, :], lhsT=ringA_t[:, s, kc * 128:(kc + 1) * 128], rhs=xbf[:, kc, :],
                                    start=(kc == 0), stop=(kc == KC - 1)), r=[("rA", s), ("xbf", kc)], w=[("ps", pA)])
                            ringA.done()
                            s2 = ringA.get()
                            for kc in range(KC):
                                op("pe", lambda s2=s2, kc=kc, pB=pB: nc.tensor.matmul(
                                    pss[pB][:, :], lhsT=ringA_t[:, s2, kc * 128:(kc + 1) * 128], rhs=xbf[:, kc, :],
                                    start=(kc == 0), stop=(kc == KC - 1)), r=[("rA", s2), ("xbf", kc)], w=[("ps", pB)])
                            ringA.done()
                            sbi = stage_b()
                            op("dve", lambda pA=pA: nc.vector.tensor_tensor(out=rA[:, 0, :], in0=pss[pA][:, :], in1=ctt[:, 0, :],
                                                                           op=ALU.mult), r=[("ps", pA), ("ctt", 0)], w=[("rA_", 0)])
                            op("dve", lambda pB=pB: nc.vector.tensor_tensor(out=rA[:, 1, :], in0=pss[pB][:, :], in1=ctt[:, 1, :],
                                                                           op=ALU.mult), r=[("ps", pB), ("ctt", 1)], w=[("rA_", 1)])
                            op("dve", lambda sbi=sbi: nc.vector.tensor_tensor(out=stb[:, sbi, :], in0=rA[:, 0, :], in1=rA[:, 1, :],
                                                                             op=ALU.add), r=[("rA_", 0), ("rA_", 1)], w=[("stb", sbi)])
                            op("sp", lambda e=e, sbi=sbi, tsl=tsl: nc.sync.dma_start(out=QKT[e, :, tsl], in_=stb[:, sbi, :]),
                               r=[("stb", sbi)], dkey=("stb", sbi))
                            sfi = e % 2
                            op("act", lambda sbi=sbi, sfi=sfi: nc.scalar.activation(out=sqb[:, sfi, :],
                                                                                  in_=stb[:, sbi, :], func=AF.Square),
                               r=[("stb", sbi)], w=[("sqb", sfi)])
                            for c in range(2):
                                pn = newps()
                                op("pe", lambda sfi=sfi, c=c, pn=pn: nc.tensor.matmul(
                                    pss[pn][:, :], lhsT=blk[:, c, :], rhs=sqb[:, sfi, :],
                                    start=True, stop=True), r=[("sqb", sfi)], w=[("ps", pn)])
                                qi = ((l * 4 + h) * 2 + qk) * 2 + c
                                op("dve", lambda pn=pn, c=c: nc.vector.tensor_reduce(out=mx[:, c:c + 1], in_=pss[pn][:, :],
                                                                                   axis=AX.X, op=ALU.max),
                                   r=[("ps", pn)], w=[("mx", c)])
                                op("dve", lambda qi=qi, c=c: nc.vector.tensor_tensor(out=qkmax[:, qi:qi + 1], in0=qkmax[:, qi:qi + 1],
                                                                                   in1=mx[:, c:c + 1], op=ALU.max),
                                   r=[("mx", c), ("qkmax", qi)], w=[("qkmax", qi)])
                        elif 8 <= e < 12 or 24 <= e < 28:
                            h = e % 4
                            s = ringA.get()
                            pi = newps()
                            for sub in range(4):
                                for kc in range(KC):
                                    op("pe", lambda s=s, kc=kc, sub=sub, pi=pi: nc.tensor.matmul(
                                        pss[pi][:, sub * 128:(sub + 1) * 128], lhsT=xbf[:, kc, sub * 128:(sub + 1) * 128],
                                        rhs=ringA_t[:, s, kc * 128:(kc + 1) * 128], start=(kc == 0), stop=(kc == KC - 1)),
                                       r=[("rA", s)] + xb_all, w=[("ps", pi)])
                            ringA.done()
                            if e < 12:
                                op("dve", lambda pi=pi, h=h: nc.vector.tensor_copy(
                                    out=vst[:, :, h, 0:128], in_=pss[pi][:, :].rearrange("p (s c) -> p s c", s=4)),
                                   r=[("ps", pi), "vst"], w=[("vst", h)])
                                op("sp", lambda t=t, h=h: nc.sync.dma_start(
                                    out=VS[h, t * 4:(t + 1) * 4, :, :].rearrange("s p c -> p s c"), in_=vst[:, :, h, :]),
                                   r=[("vst", h)], dkey=("vst", h))
                            else:
                                op("dve", lambda pi=pi, h=h: nc.vector.tensor_copy(
                                    out=hist[:, :, h * 128:(h + 1) * 128], in_=pss[pi][:, :].rearrange("p (s c) -> p s c", s=4)),
                                   r=[("ps", pi)], w=[("hist", h)])
                                if h == 3:
                                    op("sp", lambda t=t: nc.sync.dma_start(
                                        out=HI[t * T:(t + 1) * T, :].rearrange("(s p) e -> p s e", p=128), in_=hist[:, :, :]),
                                       r=[("hist", hh) for hh in range(4)], dkey="hist")
                        else:
                            h = e % 4
                            pi = newps()
                            mm_fm(ringA, ringA_t, KC, lambda kc: xbf[:, kc, :], xb_all, pi)
                            if 12 <= e < 16 or e >= 28:
                                dst = HQ if e < 16 else GATE
                                sbi = stage_b()
                                op("act", lambda pi=pi, sbi=sbi: nc.scalar.activation(out=stb[:, sbi, :], in_=pss[pi][:, :], func=AF.Silu),
                                   r=[("ps", pi)], w=[("stb", sbi)])
                                op("sp", lambda dst=dst, h=h, sbi=sbi, tsl=tsl: nc.sync.dma_start(out=dst[h, :, tsl], in_=stb[:, sbi, :]),
                                   r=[("stb", sbi)], dkey=("stb", sbi))
                            else:
                                dst = ZF if e < 20 else ZB
                                sfi = stage_f()
                                op("dve", lambda pi=pi, sfi=sfi: nc.vector.tensor_copy(out=stf[:, sfi, :], in_=pss[pi][:, :]),
                                   r=[("ps", pi)], w=[("stf", sfi)])
                                op("sp", lambda dst=dst, h=h, sfi=sfi, tsl=tsl: nc.sync.dma_start(out=dst[h, :, tsl], in_=stf[:, sfi, :]),
                                   r=[("stf", sfi)], dkey=("stf", sfi))
            sch.barrier()

    def attention_phase(l):
        with ExitStack() as cs_:
            kt_t = sb(cs_, "kt_t", [128, 2, NT], BF16)
            qt_t = sb(cs_, "qt_t", [128, 2, NT], BF16)
            v_t = sb(cs_, "v_t", [128, 2, NKT, 129], BF16)
            pt = sb(cs_, "pt", [128, 2, 2 * T], BF16)
            biasT = sb(cs_, "biasT", [128, 2, 2, NKT * NTILE], F32)
            osb = sb(cs_, "osb", [128, 4, 2, 129], F32)
            rec = sb(cs_, "rec", [128, 2, 2], F32)
            t1 = sb(cs_, "t1", [128, 2, 128], F32)
            ob = sb(cs_, "ob", [128, 2, 128], F32)
            junk = sb(cs_, "junk", [128, 128], F32)
            ssq = sb(cs_, "ssq", [128, 2], F32)
            aob = sb(cs_, "aob", [128, 4, 128], BF16)
            aost = sb(cs_, "aost", [128, 2, T], BF16)
            mtmp = sb(cs_, "mtmp", [128, 2], F32)
            psS = [pst(cs_, "aS%d" % i, [128, 2 * T], F32) for i in range(2)]
            psO = [pst(cs_, "aO%d" % i, [128, T], F32) for i in range(3)]
            psT = pst(cs_, "aT", [128, T], BF16)
            a, v, g = nc.scalar, nc.vector, nc.gpsimd
            def head_loads(h):
                hb = h % 2
                op("sp", lambda h=h, hb=hb: nc.sync.dma_start(out=kt_t[:, hb, :], in_=QKT[4 + h, :, :]), w=[("kt", hb)], dkey=("kt", hb))
                op("sp", lambda h=h, hb=hb: nc.sync.dma_start(out=qt_t[:, hb, :], in_=QKT[h, :, :]), w=[("qt", hb)], dkey=("qt", hb))
                op("sp", lambda h=h, hb=hb: nc.sync.dma_start(out=v_t[:, hb, :, :], in_=VS[h, :, :, :].rearrange("k p c -> p k c")),
                   w=[("vt", hb)], dkey=("vt", hb))

            head_loads(0)
            for h in range(4):
                hb = h % 2
                if h + 1 < 4:
                    head_loads(h + 1)
                qi = ((l * 4 + h) * 2 + 0) * 2
                ki = ((l * 4 + h) * 2 + 1) * 2
                ni = (l * 4 + h) * 2
                op("dve", lambda qi=qi: v.tensor_tensor(out=mtmp[:, 0:1], in0=qkmax[:, qi:qi + 1], in1=qkmax[:, qi + 1:qi + 2], op=ALU.max),
                   w=[("mtmp", 0)])
                op("dve", lambda ki=ki: v.tensor_tensor(out=mtmp[:, 1:2], in0=qkmax[:, ki:ki + 1], in1=qkmax[:, ki + 1:ki + 2], op=ALU.max),
                   w=[("mtmp", 1)])
                op("dve", lambda: v.tensor_tensor(out=mtmp[:, 0:1], in0=mtmp[:, 0:1], in1=mtmp[:, 1:2], op=ALU.mult),
                   r=[("mtmp", 0), ("mtmp", 1)], w=[("mtmp", 0)])
                op("pool", lambda: g.tensor_tensor(out=mtmp[:, 0:1], in0=mtmp[:, 0:1], in1=mhalf[:, 0:1], op=ALU.pow),
                   r=[("mtmp", 0)], w=[("mtmp", 0)])
                op("dve", lambda: v.reciprocal(out=mtmp[:, 0:1], in_=mtmp[:, 0:1]), r=[("mtmp", 0)], w=[("mtmp", 0)])
                op("dve", lambda ni=ni: v.tensor_scalar(out=negM[:, ni:ni + 1], in0=mtmp[:, 0:1], scalar1=-0.125,
                                                       scalar2=None, op0=ALU.mult), r=[("mtmp", 0)], w=[("negM", ni)])
                op("dve", lambda ni=ni, hb=hb: v.tensor_scalar(out=biasT[:, hb, 0, :], in0=amask[:, :], scalar1=negM[:, ni:ni + 1],
                                                              scalar2=None, op0=ALU.add), r=[("negM", ni)], w=[("biasT", hb)])

                def emit_qk(qb, kt, hb=hb):
                    buf = kt % 2
                    ksl = slice(kt * 128, (kt + 1) * 128)
                    qsl = slice(qb * T, (qb + 1) * T)
                    for c in range(2):
                        rs = slice(c * 64, (c + 1) * 64)
                        op("pe", lambda buf=buf, c=c, rs=rs, ksl=ksl, qsl=qsl: nc.tensor.matmul(
                            psS[buf][:, c * T:(c + 1) * T], lhsT=kt_t[rs, hb, ksl], rhs=qt_t[rs, hb, qsl], start=True, stop=True),
                           r=[("kt", hb), ("qt", hb)], w=[("S", buf)])

                its = [(qb, kt) for qb in range(NTILE) for kt in range(NKT)]
                emit_qk(*its[0])
                if len(its) > 1:
                    emit_qk(*its[1])
                pending = []
                for ii, (qb, kt) in enumerate(its):
                    qsl = slice(qb * T, (qb + 1) * T)
                    buf = kt % 2
                    bi = kt * NTILE + qb
                    if pending and kt == min(6, NKT - 1):
                        for tl in pending:
                            tl()
                        pending = []
                    op("act", lambda buf=buf, bi=bi, hb=hb: a.activation(
                        out=pt[:, buf, :], in_=psS[buf][:, :], func=AF.Exp, scale=0.125, bias=biasT[:, hb, 0, bi:bi + 1]),
                       r=[("S", buf), ("biasT", hb)], w=[("pt", buf)])
                    if ii + 2 < len(its):
                        emit_qk(*its[ii + 2])
                    for c in range(2):
                        for s in range(4):
                            ai = c * 4 + s
                            bk, off = ai // 3, (ai % 3) * 129
                            op("pe", lambda buf=buf, c=c, s=s, bk=bk, off=off, kt=kt, ai=ai, hb=hb: nc.tensor.matmul(
                                psO[bk][:, off:off + 129], lhsT=pt[:, buf, c * T + s * 128:c * T + (s + 1) * 128], rhs=v_t[:, hb, kt, :],
                                start=(kt == 0 and ai % 3 == 0), stop=(kt == NKT - 1), skip_group_check=True),
                               r=[("pt", buf), ("vt", hb)], w=[("O", bk)])
                    if kt != NKT - 1:
                        continue
                    ab = qb % 2
                    for s in range(4):
                        for c in range(2):
                            ai = c * 4 + s
                            bk, off = ai // 3, (ai % 3) * 129
                            op("dve", lambda s=s, c=c, bk=bk, off=off: v.tensor_copy(out=osb[:, s, c, :], in_=psO[bk][:, off:off + 129]),
                               r=[("O", bk)], w=[("osb", s, c)])
                    for s in range(4):
                        sp_ = s % 2
                        op("dve", lambda sp_=sp_, s=s: v.reciprocal(out=rec[:, sp_, :], in_=osb[:, s, :, 128]),
                           r=[("osb", s, 0), ("osb", s, 1)], w=[("rec", sp_)])
                        op("dve", lambda sp_=sp_: v.tensor_scalar(out=rec[:, sp_, 1:2], in0=rec[:, sp_, 1:2], scalar1=lam[:, l:l + 1],
                                                                 scalar2=None, op0=ALU.mult), r=[("rec", sp_)], w=[("rec", sp_)])
                        op("dve", lambda sp_=sp_, s=s: v.tensor_scalar(out=t1[:, sp_, :], in0=osb[:, s, 1, 0:128], scalar1=rec[:, sp_, 1:2],
                                                                      scalar2=None, op0=ALU.mult), r=[("rec", sp_), ("osb", s, 1)], w=[("t1", sp_)])
                        op("dve", lambda sp_=sp_, s=s: v.scalar_tensor_tensor(out=ob[:, sp_, :], in0=osb[:, s, 0, 0:128], scalar=rec[:, sp_, 0:1],
                                                                             in1=t1[:, sp_, :], op0=ALU.mult, op1=ALU.subtract),
                           r=[("rec", sp_), ("osb", s, 0), ("t1", sp_)], w=[("ob", sp_)])
                        op("dve", lambda sp_=sp_: v.scalar_tensor_tensor(out=junk[:, :], in0=ob[:, sp_, :], scalar=1.0, in1=ob[:, sp_, :],
                                                                        op0=ALU.mult, op1=ALU.mult, accum_out=ssq[:, sp_:sp_ + 1]),
                           r=[("ob", sp_)], w=["junk", ("ssq", sp_)])
                        op("dve", lambda sp_=sp_: v.tensor_scalar(out=ssq[:, sp_:sp_ + 1], in0=ssq[:, sp_:sp_ + 1], scalar1=1.0 / 128.0,
                                                                 scalar2=EPS, op0=ALU.mult, op1=ALU.add), r=[("ssq", sp_)], w=[("ssq", sp_)])
                        op("pool", lambda sp_=sp_: g.tensor_tensor(out=ssq[:, sp_:sp_ + 1], in0=ssq[:, sp_:sp_ + 1], in1=mhalf[:, 0:1], op=ALU.pow),
                           r=[("ssq", sp_)], w=[("ssq", sp_)])
                        op("dve", lambda sp_=sp_, s=s: v.scalar_tensor_tensor(out=aob[:, s, :], in0=ob[:, sp_, :], scalar=ssq[:, sp_:sp_ + 1],
                                                                             in1=agb[:, l * 128:(l + 1) * 128], op0=ALU.mult, op1=ALU.mult),
                           r=[("ob", sp_), ("ssq", sp_)], w=[("aob", s)])

                    def tail(ab=ab, h=h, qsl=qsl):
                        for s in range(4):
                            op("pe", lambda s=s: nc.tensor.transpose(psT[:, s * 128:(s + 1) * 128], aob[:, s, :], ident_bf[:, :]),
                               r=[("aob", s)], w=["psT"])
                        op("dve", lambda: v.tensor_copy(out=aost[:, ab, :], in_=psT[:, :]), r=["psT"], w=[("aost", ab)])
                        op("sp", lambda: nc.sync.dma_start(out=MIXT[h, :, qsl], in_=aost[:, ab, :]),
                           r=[("aost", ab)], dkey=("aost", ab))
                    pending.append(tail)
                for tl in pending:
                    tl()
                pending = []
            sch.barrier()

    def hgrn_phase(l):
        with ExitStack() as cs_:
            a, v, g = nc.scalar, nc.vector, nc.gpsimd
            z_t = sb(cs_, "z_t", [128, 2, 4, T], F32)
            hq_t = sb(cs_, "hq_t", [128, 2, 4, T], BF16)
            hi_t = sb(cs_, "hi_t", [64, 2, 8, 512], BF16)
            gt_t = sb(cs_, "gt_t", [128, 2, 4, T], BF16)
            of_t = sb(cs_, "of_t", [128, 2, 4, T], F32)
            e_t = sb(cs_, "e_t", [128, 2, T], F32)
            l1_t = sb(cs_, "l1_t", [128, 2, T], F32)
            l2_t = sb(cs_, "l2_t", [128, 2, T], F32)
            b_t = sb(cs_, "b_t", [128, 2, T], F32)
            eb_t = sb(cs_, "eb_t", [128, 8, T], F32)
            w_t = sb(cs_, "w_t", [128, 2, T], F32)
            Qt = sb(cs_, "Qt", [128, 8, T], BF16)
            Kt = sb(cs_, "Kt", [128, 8, T], BF16)
            ATs = sb(cs_, "ATs", [64, 4, 64], BF16)
            Ktok = sb(cs_, "Ktok", [64, 4, 128], BF16)
            S_f = sb(cs_, "S_f", [128, 4, 128], F32)
            S_tmp = sb(cs_, "S_tmp", [128, 4, 128], F32)
            S_b = sb(cs_, "S_b", [128, 4, 128], BF16)
            osum = sb(cs_, "osum", [128, 2, T], F32)
            osq = sb(cs_, "osq", [128, 2, T], F32)
            rs_t = sb(cs_, "rs_t", [128, 2, T], F32)
            host = sb(cs_, "host", [128, 2, T], BF16)
            ofst = sb(cs_, "ofst", [128, 2, T], F32)
            psA = [pst(cs_, "hA%d" % i, [64, 512], F32) for i in range(1)]
            psK = [pst(cs_, "hK%d" % i, [64, 512], BF16) for i in range(1)]
            psU = [pst(cs_, "hU%d" % i, [128, 512], F32) for i in range(1)]
            psOo = [pst(cs_, "hO%d" % i, [128, T], F32) for i in range(4)]
            psR = pst(cs_, "hR", [128, T], F32)
            for d in range(2):
                Zs = ZF if d == 0 else ZB
                mask = maskf if d == 0 else maskb
                for h in range(4):
                    op("pool", lambda h=h: g.memset(S_f[:, h, :], 0.0), w=[("S_f", h)])
                    op("pool", lambda h=h: g.memset(S_b[:, h, :], 0.0), w=[("S_b", h)])
                tiles = list(range(NTILE)) if d == 0 else list(range(NTILE - 1, -1, -1))

                def emit_loads(ti, Zs=Zs, d=d, tiles=tiles):
                    t = tiles[ti]
                    tb = ti % 2
                    tsl = slice(t * T, (t + 1) * T)
                    op("sp", lambda tb=tb, tsl=tsl, Zs=Zs: nc.sync.dma_start(out=z_t[:, tb, :, :], in_=Zs[:, :, tsl].rearrange("h p t -> p h t")),
                       w=[("z_t", tb)], dkey=("z_t", tb))
                    op("sp", lambda tb=tb, tsl=tsl: nc.sync.dma_start(out=hq_t[:, tb, :, :], in_=HQ[:, :, tsl].rearrange("h p t -> p h t")),
                       w=[("hq_t", tb)], dkey=("hq_t", tb))
                    op("sp", lambda tb=tb, t=t: nc.sync.dma_start(out=hi_t[:, tb, :, :], in_=HI[t * T:(t + 1) * T, :].rearrange("(c p) e -> p c e", p=64)),
                       w=[("hi_t", tb)], dkey=("hi_t", tb))
                    if d == 1:
                        op("sp", lambda tb=tb, tsl=tsl: nc.sync.dma_start(out=gt_t[:, tb, :, :], in_=GATE[:, :, tsl].rearrange("h p t -> p h t")),
                           w=[("gt_t", tb)], dkey=("gt_t", tb))
                        op("sp", lambda tb=tb, tsl=tsl: nc.sync.dma_start(out=of_t[:, tb, :, :], in_=OFWD[:, :, tsl].rearrange("h p t -> p h t")),
                           w=[("of_t", tb)], dkey=("of_t", tb))

                def emit_prep(ti, h, part, d=d, tiles=tiles):
                    tb = ti % 2
                    hb = h % 2
                    li = (l * 2 + d) * 4 + h
                    z = z_t[:, tb, h, :]
                    if part == 0:
                        op("act", lambda z=z, hb=hb: a.activation(out=e_t[:, hb, :], in_=z, func=AF.Exp, scale=-1.0),
                           r=[("z_t", tb)], w=[("e_t", hb)])
                        op("act", lambda hb=hb, li=li: a.activation(out=l1_t[:, hb, :], in_=e_t[:, hb, :], func=AF.Ln,
                                                                   scale=lbt[:, li:li + 1], bias=1.0), r=[("e_t", hb)], w=[("l1_t", hb)])
                        op("act", lambda hb=hb: a.activation(out=l2_t[:, hb, :], in_=e_t[:, hb, :], func=AF.Ln, scale=1.0, bias=1.0),
                           r=[("e_t", hb)], w=[("l2_t", hb)])
                        op("dve", lambda hb=hb: v.tensor_tensor(out=l1_t[:, hb, :], in0=l1_t[:, hb, :], in1=l2_t[:, hb, :], op=ALU.subtract),
                           r=[("l1_t", hb), ("l2_t", hb)], w=[("l1_t", hb)])
                        if d == 0:
                            op("dve", lambda hb=hb: v.tensor_tensor_scan(out=b_t[:, hb, :], data0=rmask[:, :], data1=l1_t[:, hb, :], initial=0.0,
                                                                        op0=ALU.mult, op1=ALU.add), r=[("l1_t", hb)], w=[("b_t", hb)])
                        else:
                            op("dve", lambda hb=hb: v.tensor_tensor_scan(out=b_t[:, hb, ::-1], data0=rmask[:, :], data1=l1_t[:, hb, ::-1], initial=0.0,
                                                                        op0=ALU.mult, op1=ALU.add), r=[("l1_t", hb)], w=[("b_t", hb)])
                    else:
                        op("act", lambda tb=tb, hb=hb, h=h: a.activation(out=eb_t[:, tb * 4 + h, :], in_=b_t[:, hb, :], func=AF.Exp),
                           r=[("b_t", hb)], w=[("eb_t", tb, h)])
                        op("dve", lambda h=h, tb=tb: v.tensor_tensor(out=Qt[:, tb * 4 + h, :], in0=hq_t[:, tb, h, :], in1=eb_t[:, tb * 4 + h, :], op=ALU.mult),
                           r=[("hq_t", tb), ("eb_t", tb, h)], w=[("Qt", tb, h)])
                        op("pool", lambda hb=hb, z=z: g.tensor_tensor(out=w_t[:, hb, :], in0=z, in1=l2_t[:, hb, :], op=ALU.add),
                           r=[("z_t", tb), ("l2_t", hb)], w=[("w_t", hb)])
                        op("dve", lambda hb=hb: v.tensor_tensor(out=w_t[:, hb, :], in0=w_t[:, hb, :], in1=b_t[:, hb, :], op=ALU.add),
                           r=[("w_t", hb), ("b_t", hb)], w=[("w_t", hb)])
                        op("act", lambda tb=tb, hb=hb, h=h, li=li: a.activation(out=Kt[:, tb * 4 + h, :], in_=w_t[:, hb, :], func=AF.Exp, scale=-1.0,
                                                                        bias=l1m[:, li:li + 1]), r=[("w_t", hb)], w=[("Kt", tb, h)])

                emit_loads(0)
                for h in range(4):
                    emit_prep(0, h, 0)
                    emit_prep(0, h, 1)
                for ti, t in enumerate(tiles):
                    tb = ti % 2
                    tsl = slice(t * T, (t + 1) * T)
                    if ti + 1 < NTILE:
                        emit_loads(ti + 1)
                    if NTILE >= 2 and ti == NTILE // 2:
                        for h in range(4):
                            op("dve", lambda h=h: v.tensor_scalar(out=S_f[:, h, :], in0=S_f[:, h, :], scalar1=keep[:, 0:1], scalar2=None,
                                                                 op0=ALU.mult), r=[("S_f", h)], w=[("S_f", h)])
                            op("dve", lambda h=h: v.tensor_copy(out=S_b[:, h, :], in_=S_f[:, h, :]), r=[("S_f", h)], w=[("S_b", h)])
                    chunks = range(8) if d == 0 else range(7, -1, -1)
                    for cix, c in enumerate(chunks):
                        csl = slice(c * 64, (c + 1) * 64)
                        cend = c * 64 + 63 if d == 0 else c * 64
                        for h in range(4):
                            op("pe", lambda tb=tb, h=h, csl=csl: nc.tensor.matmul(psA[0][:, h * 64:(h + 1) * 64], lhsT=Kt[:, tb * 4 + h, csl], rhs=Qt[:, tb * 4 + h, csl],
                                                                         start=True, stop=True), r=[("Kt", tb, h), ("Qt", tb, h)], w=["psA"])
                            op("pe", lambda tb=tb, h=h, csl=csl: nc.tensor.transpose(psK[0][:, h * 128:(h + 1) * 128], Kt[:, tb * 4 + h, csl], ident_bf[:, :]),
                               r=[("Kt", tb, h)], w=["psK"])
                        for h in range(4):
                            op("dve", lambda h=h, mask=mask: v.tensor_tensor(out=ATs[:, h, :], in0=psA[0][:, h * 64:(h + 1) * 64], in1=mask[:, :],
                                                                            op=ALU.mult), r=["psA"], w=[("ATs", h)])
                            op("act", lambda h=h: a.copy(out=Ktok[:, h, :], in_=psK[0][:, h * 128:(h + 1) * 128]), r=["psK"], w=[("Ktok", h)])
                        for h in range(4):
                            vch = hi_t[:, tb, c, h * 128:(h + 1) * 128]
                            op("pe", lambda h=h, csl=csl, vch=vch: nc.tensor.matmul(psOo[h][:, csl], lhsT=vch, rhs=ATs[:, h, :], start=True, stop=False),
                               r=[("ATs", h), ("hi_t", tb)], w=[("psO", h)])
                            op("pe", lambda h=h, vch=vch: nc.tensor.matmul(psU[0][:, h * 128:(h + 1) * 128], lhsT=Ktok[:, h, :], rhs=vch, start=True, stop=True),
                               r=[("Ktok", h), ("hi_t", tb)], w=["psU"])
                        for h in range(4):
                            op("pe", lambda tb=tb, h=h, csl=csl: nc.tensor.matmul(psOo[h][:, csl], lhsT=S_b[:, h, :], rhs=Qt[:, tb * 4 + h, csl], start=False, stop=True),
                               r=[("S_b", h), ("Qt", tb, h)], w=[("psO", h)])
                        for h in range(4):
                            op("dve", lambda h=h: v.tensor_tensor(out=S_tmp[:, h, :], in0=S_f[:, h, :], in1=psU[0][:, h * 128:(h + 1) * 128], op=ALU.add),
                               r=[("S_f", h), "psU"], w=[("S_tmp", h)])
                            op("dve", lambda tb=tb, h=h, cend=cend: v.tensor_scalar(out=S_f[:, h, :], in0=S_tmp[:, h, :], scalar1=eb_t[:, tb * 4 + h, cend:cend + 1],
                                                                            scalar2=None, op0=ALU.mult), r=[("S_tmp", h), ("eb_t", tb, h)], w=[("S_f", h)])
                            op("act", lambda h=h: a.copy(out=S_b[:, h, :], in_=S_f[:, h, :]), r=[("S_f", h)], w=[("S_b", h)])
                        if ti + 1 < NTILE:
                            emit_prep(ti + 1, cix // 2, cix % 2)
                    for h in range(4):
                        hb = h % 2
                        if d == 0:
                            op("act", lambda h=h, hb=hb: a.copy(out=ofst[:, hb, :], in_=psOo[h][:, :]), r=[("psO", h)], w=[("ofst", hb)])
                            op("sp", lambda h=h, hb=hb, tsl=tsl: nc.sync.dma_start(out=OFWD[h, :, tsl], in_=ofst[:, hb, :]),
                               r=[("ofst", hb)], dkey=("ofst", hb))
                        else:
                            op("dve", lambda h=h, hb=hb, tb=tb: v.tensor_tensor(out=osum[:, hb, :], in0=psOo[h][:, :], in1=of_t[:, tb, h, :], op=ALU.add),
                               r=[("psO", h), ("of_t", tb)], w=[("osum", hb)])
                            op("act", lambda hb=hb: a.activation(out=osq[:, hb, :], in_=osum[:, hb, :], func=AF.Square),
                               r=[("osum", hb)], w=[("osq", hb)])
                            op("pe", lambda hb=hb: nc.tensor.matmul(psR[:, :], lhsT=ones128[:, :], rhs=osq[:, hb, :], start=True, stop=True),
                               r=[("osq", hb)], w=["psR"])
                            op("act", lambda hb=hb: a.activation(out=rs_t[:, hb, :], in_=psR[:, :], func=AF.Sqrt, bias=EPS, scale=1.0),
                               r=["psR"], w=[("rs_t", hb)])
                            op("dve", lambda hb=hb: v.reciprocal(out=rs_t[:, hb, :], in_=rs_t[:, hb, :]), r=[("rs_t", hb)], w=[("rs_t", hb)])
                            op("dve", lambda hb=hb: v.scalar_tensor_tensor(out=osum[:, hb, :], in0=osum[:, hb, :], scalar=hgg[:, l:l + 1],
                                                                          in1=rs_t[:, hb, :], op0=ALU.mult, op1=ALU.mult),
                               r=[("osum", hb), ("rs_t", hb)], w=[("osum", hb)])
                            op("dve", lambda hb=hb, h=h, tb=tb: v.tensor_tensor(out=host[:, hb, :], in0=osum[:, hb, :], in1=gt_t[:, tb, h, :], op=ALU.mult),
                               r=[("osum", hb), ("gt_t", tb)], w=[("host", hb)])
                            op("sp", lambda h=h, hb=hb, tsl=tsl: nc.sync.dma_start(out=MIXT[4 + h, :, tsl], in_=host[:, hb, :]),
                               r=[("host", hb)], dkey=("host", hb))
                sch.barrier()

    stop_after = [x_ for x_ in dbg if isinstance(x_, str) and x_.startswith("stop:")]
    stop_after = stop_after[0][5:] if stop_after else None
    seq = [("A0", lambda: chain_phase(None, 0, True, False)), ("B0", lambda: attention_phase(0)), ("C0", lambda: hgrn_phase(0)),
           ("DA", lambda: chain_phase(0, 1, False, False)), ("B1", lambda: attention_phase(1)), ("C1", lambda: hgrn_phase(1)),
           ("D1", lambda: chain_phase(1, None, False, True))]
    for name, fn in seq:
        if stop_after == "P0":
            break
        fn()
        if stop_after == name:
            break
    sch.barrier()
    es.close()
    return nc


def make_in_maps(inputs, NT, seq_lens):
    raise NotImplementedError


_NC_CACHE = {}


def core_aux(NT, nseq):
    NTILE = NT // T
    NKT = NT // 128
    sl = NT // nseq
    pos = (np.arange(NT) % sl).astype(np.float32)[None, :]
    am = np.zeros((NKT, NTILE), np.float32)
    for kt in range(NKT):
        for qb in range(NTILE):
            if (kt * 128) // sl != (qb * T) // sl:
                am[kt, qb] = NEG
    keep = np.array([[1.0 if nseq == 1 else 0.0]], np.float32)
    return pos, am.reshape(1, -1), keep


def kernel(x_prompt, x_sample, w_in, w_out, attn_lambda, attn_norm_g, hg_norm_g, hg_lower_bound,
           ffn_w_gate, ffn_w_up, ffn_w_down, ln_g, ln_b):
    NT = 8192
    if NT not in _NC_CACHE:
        _NC_CACHE[NT] = build(NT)
    nc = _NC_CACHE[NT]
    f = lambda a_: np.ascontiguousarray(np.asarray(a_, dtype=np.float32))
    shared = dict(w_in=f(w_in), w_out=f(w_out), attn_lambda=f(attn_lambda), attn_norm_g=f(attn_norm_g),
                  hg_norm_g=f(hg_norm_g), hg_lower_bound=f(hg_lower_bound), ffn_w_gate=f(ffn_w_gate),
                  ffn_w_up=f(ffn_w_up), ffn_w_down=f(ffn_w_down), ln_g=f(ln_g), ln_b=f(ln_b))
    xp = f(x_prompt)
    xs = f(x_sample)
    in_maps = []
    for c in range(8):
        if c < 4:
            xc = xp[2 * c:2 * c + 2].reshape(NT, D)
            pos, am, keep = core_aux(NT, 2)
        else:
            xc = xs[c - 4].reshape(NT, D)
            pos, am, keep = core_aux(NT, 1)
        m = dict(shared)
        m.update(x=np.ascontiguousarray(xc), pos=pos, amask=am, keep=keep)
        in_maps.append(m)
    res = run_bass_kernel_spmd(nc, in_maps, core_ids=list(range(8)))
    yp = np.stack([res.results[c]["y"].reshape(2, 4096, D) for c in range(4)]).reshape(8, 4096, D)
    ys = np.stack([res.results[c]["y"].reshape(8192, D) for c in range(4, 8)])
    return (yp.astype(np.float32), ys.astype(np.float32))
```

```python
import math
from contextlib import ExitStack

import numpy as np
import concourse.bass as bass
import concourse.mybir as mybir
from concourse.bass_utils import run_bass_kernel_spmd

F32 = mybir.dt.float32
BF16 = mybir.dt.bfloat16
I32 = mybir.dt.int32
AF = mybir.ActivationFunctionType
ALU = mybir.AluOpType
AX = mybir.AxisListType

D = 1024
DFF = 2816
KC = 8
FC = 22
T = 512
EPS = 1e-5
ALPHA = 4.0 ** 0.25
THETA = 500000.0
LAM_INIT = [0.8 - 0.6 * math.exp(-0.3 * l) for l in range(2)]
NEG = -30000.0


class Sch:
    def __init__(self, nc, es):
        self.nc = nc
        self.es = es
        self.engs = {"pe": nc.tensor, "act": nc.scalar, "dve": nc.vector, "pool": nc.gpsimd, "sp": nc.sync}
        self.esem = {e: es.enter_context(nc.semaphore("sem_" + e)) for e in ("pe", "act", "dve", "pool")}
        self.ecnt = {e: 0 for e in self.esem}
        self.dsem = {}
        self.dcnt = {}
        self.waited = {e: {} for e in self.engs}
        self.ops = []
        self.base = 0
        self.sig = {}
        self.lastw = {}
        self.rd = {}
        self.dma_last = {}
        self.last_on = {}
        self.nsem = 4
        self.eng_of = {}

    def op(self, eng, fn, r=(), w=(), dkey=None):
        i = self.base + len(self.ops)
        deps = set()
        for t in r:
            j = self.lastw.get(t)
            if j is not None:
                deps.add(j)
        for t in w:
            j = self.lastw.get(t)
            if j is not None:
                deps.add(j)
            rr = self.rd.get(t)
            if rr:
                deps.update(rr[0].values())
                deps.update(rr[1])
        if dkey is not None:
            j = self.dma_last.get(dkey)
            if j is not None:
                deps.add(j)
            self.dma_last[dkey] = i
        for t in r:
            rr = self.rd.setdefault(t, ({}, []))
            if dkey is not None:
                rr[1].append(i)
            else:
                rr[0][eng] = i
        for t in w:
            self.lastw[t] = i
            self.rd[t] = ({}, [])
        deps.discard(i)
        self.ops.append((eng, fn, deps, dkey))
        self.eng_of[i] = eng if dkey is None else "dma"
        if dkey is None:
            self.last_on[eng] = i
        return i

    def barrier(self):
        deps = set(self.last_on.values()) | set(self.dma_last.values())
        for e in self.engs:
            self.ops.append((e, None, set(deps), None))
        self.lastw = {}
        self.rd = {}
        self.flush()

    def flush(self):
        nc = self.nc
        needed = set()
        for (eng_, fn_, deps, dk_) in self.ops:
            if eng_ == "pe" and dk_ is None and fn_ is not None:
                needed |= {d for d in deps if self.eng_of.get(d) != "pe"}
            else:
                needed |= deps
        eng_of = {}
        for k, (eng, fn, deps, dkey) in enumerate(self.ops):
            i = self.base + k
            E = self.engs[eng]
            w8 = self.waited[eng]
            for d in sorted(deps):
                if d not in self.sig:
                    continue
                sem, val, seng, key = self.sig[d]
                if seng == "pe" and eng == "pe":
                    continue
                if w8.get(key, 0) < val:
                    E.wait_ge(sem, val)
                    w8[key] = val
            if fn is None:
                continue
            ins = fn()
            if dkey is not None:
                if dkey not in self.dsem:
                    self.dsem[dkey] = self.es.enter_context(nc.semaphore("dsem%d" % len(self.dsem)))
                    self.dcnt[dkey] = 0
                    self.nsem += 1
                self.dcnt[dkey] += 16
                ins.then_inc(self.dsem[dkey], 16)
                self.sig[i] = (self.dsem[dkey], self.dcnt[dkey], "dma", ("d", dkey))
            elif i in needed:
                self.ecnt[eng] += 1
                ins.then_inc(self.esem[eng], 1)
                self.sig[i] = (self.esem[eng], self.ecnt[eng], eng, ("e", eng))
        self.base += len(self.ops)
        self.ops = []
        if not self.lastw:
            self.sig = {k: v for k, v in self.sig.items() if k in set(self.last_on.values()) | set(self.dma_last.values())}


class WRing:
    def __init__(self, sch, nc, name, tens, nslot):
        self.sch, self.nc, self.name, self.tens, self.nslot = sch, nc, name, tens, nslot
        self.pieces = []
        self.nl = 0
        self.nu = 0

    def start(self, pieces):
        self.pieces = pieces
        self.nl = 0
        self.nu = 0
        for _ in range(self.nslot):
            self._load()

    def _load(self):
        i = self.nl
        if i >= len(self.pieces):
            return
        slot = i % self.nslot
        ap, n = self.pieces[i]
        tens, nc = self.tens, self.nc
        self.sch.op("sp", lambda: nc.sync.dma_start(out=tens[:, slot, 0:n], in_=ap),
                    w=[(self.name, slot)], dkey=(self.name, slot))
        self.nl += 1

    def get(self):
        s = self.nu % self.nslot
        self.nu += 1
        return s

    def done(self):
        self._load()


def build(NT, dbg=()):
    NTILE = NT // T
    NKT = NT // 128
    nc = bass.Bass("TRN2", target_bir_lowering=False)
    es = ExitStack()

    def din(name, shape, dt=F32):
        return nc.dram_tensor(name, shape, dt, kind="ExternalInput").ap()

    x_in = din("x", [NT, D])
    pos_in = din("pos", [1, NT])
    amask_in = din("amask", [1, NKT * NTILE])
    keep_in = din("keep", [1, 1])
    w_in = din("w_in", [2, D, 4096])
    w_out = din("w_out", [2, D, D])
    attn_lambda = din("attn_lambda", [2, 4, 64])
    attn_norm_g = din("attn_norm_g", [2, 128])
    hg_norm_g = din("hg_norm_g", [2, 128])
    hg_lb = din("hg_lower_bound", [2, 2, 512])
    w_gate = din("ffn_w_gate", [2, 2, D, DFF])
    w_up = din("ffn_w_up", [2, 2, D, DFF])
    w_down = din("ffn_w_down", [2, 2, DFF, D])
    ln_g = din("ln_g", [2, 3, D])
    ln_b = din("ln_b", [2, 3, D])
    y_out = nc.dram_tensor("y", [NT, D], F32, kind="ExternalOutput").ap()

    def scr(name, shape, dt):
        kind = "ExternalOutput" if name in dbg else "Internal"
        return nc.dram_tensor(name, shape, dt, kind=kind).ap()

    WGU = scr("WGU", [2, 2, 2, FC, 128, KC * 128], BF16)
    WD = scr("WD", [2, 2, 8, 128, FC * 128], BF16)
    WIN = scr("WIN", [2, 32, 128, KC * 128], BF16)
    WOUT = scr("WOUT", [2, 8, 128, KC * 128], BF16)
    WINS = scr("WINS", [2, 8, 128, KC * 128], BF16)
    CT = scr("CT", [128, NT], F32)
    ST = scr("ST", [128, NT], F32)
    X1T = scr("X1T", [8, 128, NT], F32)
    QKT = scr("QKT", [8, 128, NT], BF16)
    VS = scr("VS", [4, NKT, 128, 129], BF16)
    HQ = scr("HQ", [4, 128, NT], BF16)
    ZF = scr("ZF", [4, 128, NT], F32)
    ZB = scr("ZB", [4, 128, NT], F32)
    GATE = scr("GATE", [4, 128, NT], BF16)
    HI = scr("HI", [NT, 512], BF16)
    OFWD = scr("OFWD", [4, 128, NT], F32)
    MIXT = scr("MIXT", [8, 128, NT], BF16)

    sch = Sch(nc, es)
    op = sch.op

    uniq = [0]

    def sb(stack, name, shape, dt):
        uniq[0] += 1
        return stack.enter_context(nc.sbuf_tensor("%s_%d" % (name, uniq[0]), shape, dt))

    def pst(stack, name, shape, dt):
        uniq[0] += 1
        return stack.enter_context(nc.psum_tensor("%s_%d" % (name, uniq[0]), shape, dt))

    ident_bf = sb(es, "ident_bf", [128, 128], BF16)
    ident_f = sb(es, "ident_f", [128, 128], F32)
    onesm = sb(es, "onesm", [128, 128], F32)
    ones128 = sb(es, "ones128", [128, 128], F32)
    blk = sb(es, "blk", [128, 2, 128], BF16)
    maskf = sb(es, "maskf", [64, 64], F32)
    maskb = sb(es, "maskb", [64, 64], F32)
    rmask = sb(es, "rmask", [128, T], F32)
    lng = sb(es, "lng", [128, 48], F32)
    lnb = sb(es, "lnb", [128, 48], F32)
    hgg = sb(es, "hgg", [128, 2], F32)
    agb = sb(es, "agb", [128, 256], F32)
    lam = sb(es, "lam", [128, 2], F32)
    lbt = sb(es, "lbt", [128, 16], F32)
    l1m = sb(es, "l1m", [128, 16], F32)
    amask = sb(es, "amask_sb", [128, NKT * NTILE], F32)
    keep = sb(es, "keep_sb", [128, 1], F32)
    qkmax = sb(es, "qkmax", [128, 32], F32)
    mhalf = sb(es, "mhalf", [128, 4], F32)
    negM = sb(es, "negM", [128, 16], F32)

    with ExitStack() as ps:
        g = nc.gpsimd
        v = nc.vector
        a = nc.scalar

        def memset(t_, val, tok):
            op("pool", lambda: g.memset(t_, val), w=[tok])

        def affsel(t_, pattern, cmp, base, cm, tok):
            op("pool", lambda: g.affine_select(out=t_, in_=t_, pattern=pattern, compare_op=cmp, fill=0.0,
                                               base=base, channel_multiplier=cm), r=[tok], w=[tok])

        memset(ident_bf[:, :], 1.0, "ident_bf")
        affsel(ident_bf[:, :], [[-1, 128]], ALU.is_equal, 0, 1, "ident_bf")
        memset(ident_f[:, :], 1.0, "ident_f")
        affsel(ident_f[:, :], [[-1, 128]], ALU.is_equal, 0, 1, "ident_f")
        memset(onesm[:, :], 1.0 / 1024.0, "c1")
        memset(ones128[:, :], 1.0 / 128.0, "c2")
        memset(blk[:, :, :], 0.0, "blk")
        memset(blk[0:64, 0, :], 1.0, "blk")
        memset(blk[64:128, 1, :], 1.0, "blk")
        memset(maskf[:, :], 1.0, "maskf")
        affsel(maskf[:, :], [[1, 64]], ALU.is_ge, 0, -1, "maskf")
        memset(maskb[:, :], 1.0, "maskb")
        affsel(maskb[:, :], [[-1, 64]], ALU.is_ge, 0, 1, "maskb")
        memset(rmask[:, :], 1.0, "rmask")
        memset(rmask[:, ::64], 0.0, "rmask")
        memset(mhalf[:, :], -0.5, "mhalf")
        memset(qkmax[:, :], 0.0, "qkmax")

        def ld(t_, src, tok, key):
            op("sp", lambda: nc.sync.dma_start(out=t_, in_=src, allow_slow_non_contiguous=True), w=[tok], dkey=key)

        ld(lng[:, :].rearrange("p (a k) -> p a k", k=8), ln_g.rearrange("l i (k p) -> p (l i) k", p=128), "lng", "p0")
        ld(lnb[:, :].rearrange("p (a k) -> p a k", k=8), ln_b.rearrange("l i (k p) -> p (l i) k", p=128), "lnb", "p1")
        ld(hgg[:, :], hg_norm_g.rearrange("l p -> p l"), "hgg", "p2")
        ld(amask[:, :], amask_in.partition_broadcast(128), "amask", "p3")
        ld(keep[:, :], keep_in.partition_broadcast(128), "keep", "p4")
        agt = sb(ps, "agt", [128, 256], F32)
        ld(agt[:, :], attn_norm_g.rearrange("l c -> (l c)").partition_broadcast(128), "agt", "p5")
        for l in range(2):
            op("dve", lambda l=l: v.tensor_scalar(out=agb[:, l * 128:(l + 1) * 128], in0=agt[:, l * 128:(l + 1) * 128],
                                                 scalar1=1.0 - LAM_INIT[l], scalar2=None, op0=ALU.mult),
               r=["agt"], w=[("agb", l)])
        lpt = sb(ps, "lpt", [128, 512], F32)
        ld(lpt[:, :], attn_lambda.rearrange("l a c -> (l a c)").partition_broadcast(128), "lpt", "p6")
        lpp = sb(ps, "lpp", [128, 4, 64], F32)
        lps = sb(ps, "lps", [128, 4], F32)
        lpe = sb(ps, "lpe", [128, 4], F32)
        lp4 = lpt[:, :].rearrange("p (l a c) -> p l a c", l=2, a=4)
        for l in range(2):
            for k in range(2):
                op("dve", lambda l=l, k=k: v.tensor_tensor(out=lpp[:, l * 2 + k, :], in0=lp4[:, l, 2 * k, :],
                                                          in1=lp4[:, l, 2 * k + 1, :], op=ALU.mult),
                   r=["lpt"], w=[("lpp", l, k)])
        op("dve", lambda: v.tensor_reduce(out=lps[:, :], in_=lpp[:, :, :], axis=AX.X, op=ALU.add),
           r=[("lpp", l, k) for l in range(2) for k in range(2)], w=["lps"])
        op("act", lambda: a.activation(out=lpe[:, :], in_=lps[:, :], func=AF.Exp), r=["lps"], w=["lpe"])
        for l in range(2):
            op("dve", lambda l=l: v.tensor_tensor(out=lam[:, l:l + 1], in0=lpe[:, 2 * l:2 * l + 1],
                                                 in1=lpe[:, 2 * l + 1:2 * l + 2], op=ALU.subtract),
               r=["lpe"], w=[("lam", l)])
            op("dve", lambda l=l: v.tensor_scalar(out=lam[:, l:l + 1], in0=lam[:, l:l + 1], scalar1=LAM_INIT[l],
                                                 scalar2=None, op0=ALU.add), r=[("lam", l)], w=[("lam", l)])
        lbi = sb(ps, "lbi", [128, 16], F32)
        ld(lbi[:, :].rearrange("p (a h) -> p a h", h=4), hg_lb.rearrange("d l (h p) -> p (d l) h", p=128), "lbi", "p7")
        lbm = sb(ps, "lbm", [128, 2, 4], F32)
        lbe = sb(ps, "lbe", [128, 16], F32)
        lbs_ = sb(ps, "lbs_", [128, 2, 4], F32)
        lbi4 = lbi[:, :].rearrange("p (d l h) -> p d l h", d=2, l=2)
        lbe4 = lbe[:, :].rearrange("p (d l h) -> p d l h", d=2, l=2)
        op("dve", lambda: v.tensor_tensor(out=lbm[:, :, :], in0=lbi4[:, :, 0, :], in1=lbi4[:, :, 1, :], op=ALU.max),
           r=["lbi"], w=["lbm"])
        for l in range(2):
            op("dve", lambda l=l: v.tensor_tensor(out=lbe4[:, :, l, :], in0=lbi4[:, :, l, :], in1=lbm[:, :, :],
                                                 op=ALU.subtract), r=["lbi", "lbm"], w=[("lbe", l)])
        op("act", lambda: a.activation(out=lbe[:, :], in_=lbe[:, :], func=AF.Exp),
           r=[("lbe", 0), ("lbe", 1)], w=[("lbe", 0), ("lbe", 1)])
        op("dve", lambda: v.tensor_tensor(out=lbs_[:, :, :], in0=lbe4[:, :, 0, :], in1=lbe4[:, :, 1, :], op=ALU.add),
           r=[("lbe", 0), ("lbe", 1)], w=["lbs_"])
        op("dve", lambda: v.reciprocal(out=lbs_[:, :, :], in_=lbs_[:, :, :]), r=["lbs_"], w=["lbs_"])
        for l in range(2):
            op("dve", lambda l=l: v.tensor_tensor(out=lbe4[:, :, l, :], in0=lbe4[:, :, l, :], in1=lbs_[:, :, :],
                                                 op=ALU.mult), r=[("lbe", l), "lbs_"], w=[("lbe", l)])
        lbc = sb(ps, "lbc", [128, 2, 4], F32)
        lbt4 = lbt[:, :].rearrange("p (l d h) -> p l d h", l=2, d=2)
        op("dve", lambda: v.tensor_tensor(out=lbt4[:, 0, :, :], in0=lbe4[:, :, 0, :], in1=lbe4[:, :, 0, :],
                                          op=ALU.subtract), r=[("lbe", 0)], w=[("lbt", 0)])
        op("dve", lambda: v.tensor_tensor(out=lbc[:, :, :], in0=lbe4[:, :, 0, :], in1=lbe4[:, :, 1, :], op=ALU.add),
           r=[("lbe", 0), ("lbe", 1)], w=["lbc"])
        op("dve", lambda: v.tensor_tensor(out=lbt4[:, 1, :, :], in0=lbc[:, :, :], in1=lbe4[:, :, 0, :],
                                          op=ALU.subtract), r=["lbc", ("lbe", 0)], w=[("lbt", 1)])
        op("dve", lambda: v.tensor_scalar(out=lbt[:, :], in0=lbt[:, :], scalar1=0.0, scalar2=None, op0=ALU.max),
           r=[("lbt", 0), ("lbt", 1)], w=["lbt"])
        op("act", lambda: a.activation(out=l1m[:, :], in_=lbt[:, :], func=AF.Ln, scale=-1.0, bias=1.0),
           r=["lbt"], w=["l1m"])

        pidx = sb(ps, "pidx", [128, 1], I32)
        pjf = sb(ps, "pjf", [128, 4], F32)
        invf = sb(ps, "invf", [128, 1], F32)
        rotm = sb(ps, "rotm", [128, 1], F32)
        sgn = sb(ps, "sgn", [128, 1], F32)
        tmp1 = sb(ps, "tmp1", [128, 1], F32)
        op("pool", lambda: g.iota(pidx[:, :], pattern=[[0, 1]], base=0, channel_multiplier=1), w=["pidx"])
        op("dve", lambda: v.tensor_copy(out=pjf[:, 1:2], in_=pidx[:, :]), r=["pidx"], w=["pjf"])
        op("dve", lambda: v.tensor_scalar(out=pjf[:, 3:4], in0=pjf[:, 1:2], scalar1=64.0, scalar2=-64.0, op0=ALU.is_ge,
                                          op1=ALU.mult), r=["pjf"], w=["pjf3"])
        op("dve", lambda: v.tensor_tensor(out=pjf[:, 1:2], in0=pjf[:, 1:2], in1=pjf[:, 3:4], op=ALU.add),
           r=["pjf", "pjf3"], w=["pjf"])
        op("dve", lambda: v.tensor_copy(out=pjf[:, 2:3], in_=pjf[:, 1:2]), r=["pjf"], w=["pjf2"])
        for m in (16.0, 32.0, 48.0):
            op("dve", lambda m=m: v.tensor_scalar(out=pjf[:, 3:4], in0=pjf[:, 1:2], scalar1=m, scalar2=-16.0, op0=ALU.is_ge,
                                                  op1=ALU.mult), r=["pjf"], w=["pjf3"])
            op("dve", lambda: v.tensor_tensor(out=pjf[:, 2:3], in0=pjf[:, 2:3], in1=pjf[:, 3:4], op=ALU.add),
               r=["pjf2", "pjf3"], w=["pjf2"])
        op("dve", lambda: v.tensor_scalar(out=pjf[:, 3:4], in0=pjf[:, 2:3], scalar1=8.0, scalar2=-8.0, op0=ALU.is_ge,
                                          op1=ALU.mult), r=["pjf2"], w=["pjf3"])
        op("dve", lambda: v.tensor_tensor(out=pjf[:, 0:1], in0=pjf[:, 2:3], in1=pjf[:, 3:4], op=ALU.add),
           r=["pjf2", "pjf3"], w=["pjf0"])
        op("dve", lambda: v.memset(invf[:, :], 0.0), w=["invf"])
        for i in range(8):
            ci = float(np.float32(THETA) ** np.float32(-i * 2.0 / 16.0))
            op("dve", lambda i=i, ci=ci: v.tensor_scalar(out=tmp1[:, :], in0=pjf[:, 0:1], scalar1=float(i), scalar2=ci,
                                                        op0=ALU.is_equal, op1=ALU.mult), r=["pjf0"], w=["tmp1"])
            op("dve", lambda: v.tensor_tensor(out=invf[:, :], in0=invf[:, :], in1=tmp1[:, :], op=ALU.add),
               r=["tmp1", "invf"], w=["invf"])
        op("dve", lambda: v.tensor_single_scalar(out=rotm[:, :], in_=pjf[:, 1:2], scalar=16.0, op=ALU.is_lt),
           r=["pjf"], w=["rotm"])
        op("dve", lambda: v.tensor_scalar(out=sgn[:, :], in0=pjf[:, 2:3], scalar1=8.0, scalar2=None, op0=ALU.is_ge),
           r=["pjf2"], w=["sgn"])
        op("dve", lambda: v.tensor_scalar(out=sgn[:, :], in0=sgn[:, :], scalar1=2.0, scalar2=-1.0, op0=ALU.mult,
                                          op1=ALU.add), r=["sgn"], w=["sgn"])
        op("dve", lambda: v.tensor_tensor(out=sgn[:, :], in0=sgn[:, :], in1=rotm[:, :], op=ALU.mult),
           r=["sgn", "rotm"], w=["sgn"])
        onem = sb(ps, "onem", [128, 1], F32)
        op("dve", lambda: v.tensor_scalar(out=onem[:, :], in0=rotm[:, :], scalar1=-1.0, scalar2=1.0, op0=ALU.mult,
                                          op1=ALU.add), r=["rotm"], w=["onem"])
        PW = min(NT, 2048)
        posb = sb(ps, "posb", [128, PW], F32)
        ua = sb(ps, "ua", [128, PW], F32)
        ub = sb(ps, "ub", [128, PW], F32)
        ui = sb(ps, "ui", [128, PW], I32)
        uc = sb(ps, "uc", [128, PW], F32)
        cs = sb(ps, "cs", [128, 2, PW], F32)
        for pc in range(NT // PW):
            sl = slice(pc * PW, (pc + 1) * PW)
            ld(posb[:, :], pos_in[:, sl].partition_broadcast(128), "posb", "p8")
            op("dve", lambda: v.tensor_scalar(out=ua[:, :], in0=posb[:, :], scalar1=invf[:, 0:1],
                                              scalar2=float(1.0 / (2.0 * math.pi)), op0=ALU.mult, op1=ALU.mult),
               r=["posb", "invf"], w=["ua"])
            for which in range(2):
                if which == 0:
                    op("dve", lambda: v.tensor_scalar(out=ub[:, :], in0=ua[:, :], scalar1=0.25, scalar2=None,
                                                      op0=ALU.add), r=["ua"], w=["ub"])
                else:
                    op("dve", lambda: v.tensor_copy(out=ub[:, :], in_=ua[:, :]), r=["ua"], w=["ub"])
                op("dve", lambda: v.tensor_copy(out=ui[:, :], in_=ub[:, :]), r=["ub"], w=["ui"])
                op("dve", lambda: v.tensor_copy(out=uc[:, :], in_=ui[:, :]), r=["ui"], w=["uc"])
                op("dve", lambda: v.tensor_tensor(out=ub[:, :], in0=ub[:, :], in1=uc[:, :], op=ALU.subtract),
                   r=["ub", "uc"], w=["ub"])
                op("dve", lambda: v.tensor_single_scalar(out=uc[:, :], in_=ub[:, :], scalar=0.5, op=ALU.is_gt),
                   r=["ub"], w=["uc"])
                op("dve", lambda: v.tensor_tensor(out=ub[:, :], in0=ub[:, :], in1=uc[:, :], op=ALU.subtract),
                   r=["ub", "uc"], w=["ub"])
                op("dve", lambda: v.tensor_single_scalar(out=uc[:, :], in_=ub[:, :], scalar=-0.5, op=ALU.is_lt),
                   r=["ub"], w=["uc"])
                op("dve", lambda: v.tensor_tensor(out=ub[:, :], in0=ub[:, :], in1=uc[:, :], op=ALU.add),
                   r=["ub", "uc"], w=["ub"])
                op("act", lambda which=which: a.activation(out=cs[:, which, :], in_=ub[:, :], func=AF.Sin,
                                                           scale=float(2.0 * math.pi)),
                   r=["ub"], w=[("cs", which)])
            op("dve", lambda: v.tensor_scalar(out=cs[:, 0, :], in0=cs[:, 0, :], scalar1=rotm[:, 0:1],
                                              scalar2=onem[:, 0:1], op0=ALU.mult, op1=ALU.add),
               r=[("cs", 0), "rotm", "onem"], w=[("cs", 0)])
            op("dve", lambda: v.tensor_scalar(out=cs[:, 1, :], in0=cs[:, 1, :], scalar1=sgn[:, 0:1], scalar2=None,
                                              op0=ALU.mult), r=[("cs", 1), "sgn"], w=[("cs", 1)])
            op("sp", lambda sl=sl: nc.sync.dma_start(out=CT[:, sl], in_=cs[:, 0, :]), r=[("cs", 0)], dkey="p9")
            op("sp", lambda sl=sl: nc.sync.dma_start(out=ST[:, sl], in_=cs[:, 1, :]), r=[("cs", 1)], dkey="p10")
        sch.barrier()

    with ExitStack() as ps:
        stage = sb(ps, "wstage", [128, 2, 11264], F32)
        pct = sb(ps, "wpct", [128, 6, FC * 128], BF16)
        pcs = sb(ps, "wpcs", [128, 2, KC * 128], BF16)
        jobs = []
        for l in range(2):
            for i in range(2):
                for gu, wsrc in enumerate((w_gate, w_up)):
                    for half in range(2):
                        jobs.append((wsrc[l, i], KC, half * 1408, 1408,
                                     lambda j, l=l, i=i, gu=gu, half=half: WGU[l, i, gu, half * 11 + j]))
                for blk_ in range(4):
                    jobs.append((w_down[l, i], FC, blk_ * 256, 256,
                                 lambda j, l=l, i=i, blk_=blk_: WD[l, i, blk_ * 2 + j]))
            for blk_ in range(4):
                jobs.append((w_in[l], KC, blk_ * 1024, 1024, lambda j, l=l, blk_=blk_: WIN[l, blk_ * 8 + j],
                             (lambda j, l=l: WINS[l, j]) if blk_ == 0 else None))
            jobs.append((w_out[l], KC, 0, 1024, lambda j, l=l: WOUT[l, j]))
        cnt = 0
        engs3 = ("dve", "pool", "act")
        swc = 0
        for jb, job in enumerate(jobs):
            src, kcw, c0, ncol, dst = job[:5]
            dsts = job[5] if len(job) > 5 else None
            sslot = jb % 2
            sview = stage[:, sslot, 0:kcw * ncol].rearrange("p (k c) -> p k c", k=kcw)
            op("sp", lambda src=src, c0=c0, ncol=ncol, sview=sview: nc.sync.dma_start(
                out=sview, in_=src[:, c0:c0 + ncol].rearrange("(k p) c -> p k c", p=128)),
               w=[("stage", sslot)], dkey=("stage", sslot))
            for j in range(ncol // 128):
                pslot = cnt % 6
                e = engs3[cnt % 3]
                cnt += 1
                outv = pct[:, pslot, 0:kcw * 128].rearrange("p (k c) -> p k c", k=kcw)
                inv = sview[:, :, j * 128:(j + 1) * 128]
                if e == "act":
                    op("act", lambda outv=outv, inv=inv: nc.scalar.copy(out=outv, in_=inv),
                       r=[("stage", sslot)], w=[("pct", pslot)])
                elif e == "dve":
                    op("dve", lambda outv=outv, inv=inv: nc.vector.tensor_copy(out=outv, in_=inv),
                       r=[("stage", sslot)], w=[("pct", pslot)])
                else:
                    op("pool", lambda outv=outv, inv=inv: nc.gpsimd.tensor_copy(out=outv, in_=inv),
                       r=[("stage", sslot)], w=[("pct", pslot)])
                dap = dst(j)
                op("sp", lambda dap=dap, pslot=pslot, kcw=kcw: nc.sync.dma_start(out=dap, in_=pct[:, pslot, 0:kcw * 128]),
                   r=[("pct", pslot)], dkey=("pct", pslot))
                if dsts is not None:
                    ss = swc % 2
                    swc += 1
                    w4 = pct[:, pslot, 0:KC * 128].rearrange("p (k c d) -> p k c d", k=KC, c=2)
                    s4 = pcs[:, ss, :].rearrange("p (k c d) -> p k c d", k=KC, c=2)
                    op("dve", lambda w4=w4, s4=s4: nc.vector.tensor_copy(out=s4[:, :, :, 0:8], in_=w4[:, :, :, 8:16]),
                       r=[("pct", pslot)], w=[("pcs", ss, 0)])
                    op("dve", lambda w4=w4, s4=s4: nc.vector.tensor_copy(out=s4[:, :, :, 8:16], in_=w4[:, :, :, 0:8]),
                       r=[("pct", pslot)], w=[("pcs", ss, 1)])
                    op("dve", lambda w4=w4, s4=s4: nc.vector.tensor_copy(out=s4[:, :, :, 16:64], in_=w4[:, :, :, 16:64]),
                       r=[("pct", pslot)], w=[("pcs", ss, 2)])
                    dap2 = dsts(j)
                    op("sp", lambda dap2=dap2, ss=ss: nc.sync.dma_start(out=dap2, in_=pcs[:, ss, :]),
                       r=[("pcs", ss, 0), ("pcs", ss, 1), ("pcs", ss, 2)], dkey=("pcs", ss))
        sch.barrier()

    cut = [x_ for x_ in dbg if isinstance(x_, str) and x_.startswith("cut:")]
    cut = int(cut[0][4:]) if cut else 99

    def chain_phase(do_D, do_A, first, last):
        with ExitStack() as cs_:
            xT = sb(cs_, "xT", [128, KC, T], F32)
            xbf = sb(cs_, "xbf", [128, KC, T], BF16)
            hT = sb(cs_, "hT", [128, FC, T], BF16)
            sq = sb(cs_, "sq", [128, KC, T], F32)
            zsum = sb(cs_, "zsum", [128, T], F32)
            ssum = sb(cs_, "ssum", [128, T], F32)
            mean_sb = sb(cs_, "mean_sb", [128, T], F32)
            msq = sb(cs_, "msq", [128, T], F32)
            rstd = sb(cs_, "rstd", [128, T], F32)
            sgt = sb(cs_, "sgt", [128, 2, T], F32)
            mixbf = sb(cs_, "mixbf", [128, KC, T], BF16) if do_D is not None else None
            stb = sb(cs_, "stb", [128, 4, T], BF16)
            stf = sb(cs_, "stf", [128, 4, T], F32)
            rA = sb(cs_, "rA", [128, 2, T], F32)
            vst = sb(cs_, "vst", [128, 4, 4, 129], BF16)
            hist = sb(cs_, "hist", [128, 4, 512], BF16)
            ctt = sb(cs_, "ctt", [128, 2, T], F32)
            tok = sb(cs_, "tok", [128, 4, D], F32) if (first or last) else None
            ringA_t = sb(cs_, "ringA", [128, 12, KC * 128], BF16)
            ringB_t = sb(cs_, "ringB", [128, 3, FC * 128], BF16)
            mx = sb(cs_, "mx", [128, 4], F32)
            sqb = sb(cs_, "sqb", [128, 2, T], BF16)
            pss = [pst(cs_, "cps%d" % i, [128, T], F32) for i in range(8)]
            ringA = WRing(sch, nc, "rA", ringA_t, 12)
            ringB = WRing(sch, nc, "rB", ringB_t, 3)
            psn = [0]
            stn = [0, 0]

            def newps():
                i = psn[0] % 8
                psn[0] += 1
                return i

            pa, pb = [], []
            for t in range(NTILE):
                if do_D is not None:
                    l = do_D
                    for j in range(8):
                        pa.append((WOUT[l, j], KC * 128))
                    for j in range(FC):
                        pa.append((WGU[l, 1, 0, j], KC * 128))
                        pa.append((WGU[l, 1, 1, j], KC * 128))
                    for j in range(8):
                        pb.append((WD[l, 1, j], FC * 128))
                if do_A is not None:
                    l = do_A
                    for j in range(FC):
                        pa.append((WGU[l, 0, 0, j], KC * 128))
                        pa.append((WGU[l, 0, 1, j], KC * 128))
                    for j in range(8):
                        pb.append((WD[l, 0, j], FC * 128))
                    for j in range(32):
                        pa.append((WIN[l, j], KC * 128))
                        if j < 8:
                            pa.append((WINS[l, j], KC * 128))
            if "no:ring" in dbg:
                pa, pb = [], []
            ringA.start(pa)
            ringB.start(pb)
            if do_A is not None and "no:vst" not in dbg:
                op("pool", lambda: nc.gpsimd.memset(vst[:, :, :, :], 1.0), w=["vst"])

            def mm_fm(ring, ringt, kcn, rhs_of, rtoks, pi, ptok_extra=()):
                s = ring.get()
                for kc in range(kcn):
                    op("pe", lambda s=s, kc=kc: nc.tensor.matmul(pss[pi][:, :], lhsT=ringt[:, s, kc * 128:(kc + 1) * 128],
                                                               rhs=rhs_of(kc), start=(kc == 0), stop=(kc == kcn - 1)),
                       r=[(ring.name, s)] + ([rtoks[kc]] if len(rtoks) == kcn else rtoks), w=[("ps", pi)])
                ring.done()

            def stats_acc(dc):
                op("act", lambda dc=dc: nc.scalar.activation(out=sq[:, dc, :], in_=xT[:, dc, :], func=AF.Square),
                   r=[("xT", dc)], w=[("sq", dc)])
                if dc == 0:
                    op("dve", lambda: nc.vector.tensor_copy(out=zsum[:, :], in_=xT[:, 0, :]), r=[("xT", 0)], w=["zsum"])
                    op("dve", lambda: nc.vector.tensor_copy(out=ssum[:, :], in_=sq[:, 0, :]), r=[("sq", 0)], w=["ssum"])
                else:
                    op("dve", lambda dc=dc: nc.vector.tensor_tensor(out=zsum[:, :], in0=zsum[:, :], in1=xT[:, dc, :], op=ALU.add),
                       r=[("xT", dc), "zsum"], w=["zsum"])
                    op("dve", lambda dc=dc: nc.vector.tensor_tensor(out=ssum[:, :], in0=ssum[:, :], in1=sq[:, dc, :], op=ALU.add),
                       r=[("sq", dc), "ssum"], w=["ssum"])

            def layer_norm(l, i):
                gi = (l * 3 + i) * 8
                eps_eff = EPS / (ALPHA * ALPHA)
                pm = newps()
                op("pe", lambda: nc.tensor.matmul(pss[pm][:, :], lhsT=onesm[:, :], rhs=zsum[:, :], start=True, stop=True),
                   r=["zsum"], w=[("ps", pm)])
                pv = newps()
                op("pe", lambda: nc.tensor.matmul(pss[pv][:, :], lhsT=onesm[:, :], rhs=ssum[:, :], start=True, stop=True),
                   r=["ssum"], w=[("ps", pv)])
                op("act", lambda: nc.scalar.activation(out=mean_sb[:, :], in_=pss[pm][:, :], func=AF.Identity),
                   r=[("ps", pm)], w=["mean_sb"])
                op("act", lambda: nc.scalar.activation(out=msq[:, :], in_=mean_sb[:, :], func=AF.Square),
                   r=["mean_sb"], w=["msq"])
                op("dve", lambda: nc.vector.tensor_tensor(out=rstd[:, :], in0=pss[pv][:, :], in1=msq[:, :], op=ALU.subtract),
                   r=[("ps", pv), "msq"], w=["rstd"])
                op("dve", lambda: nc.vector.tensor_scalar(out=rstd[:, :], in0=rstd[:, :], scalar1=0.0, scalar2=None, op0=ALU.max),
                   r=["rstd"], w=["rstd"])
                op("act", lambda: nc.scalar.activation(out=rstd[:, :], in_=rstd[:, :], func=AF.Sqrt, bias=eps_eff, scale=1.0),
                   r=["rstd"], w=["rstd"])
                op("dve", lambda: nc.vector.reciprocal(out=rstd[:, :], in_=rstd[:, :]), r=["rstd"], w=["rstd"])
                for k in range(KC):
                    op("dve", lambda k=k: nc.vector.tensor_tensor(out=xT[:, k, :], in0=xT[:, k, :], in1=mean_sb[:, :], op=ALU.subtract),
                       r=[("xT", k), "mean_sb"], w=[("xT", k)])
                    op("dve", lambda k=k: nc.vector.tensor_tensor(out=xT[:, k, :], in0=xT[:, k, :], in1=rstd[:, :], op=ALU.mult),
                       r=[("xT", k), "rstd"], w=[("xT", k)])
                    op("act", lambda k=k: nc.scalar.activation(out=xT[:, k, :], in_=xT[:, k, :], func=AF.Identity,
                                                               scale=lng[:, gi + k:gi + k + 1], bias=lnb[:, gi + k:gi + k + 1]),
                       r=[("xT", k)], w=[("xT", k)])
                    op("dve", lambda k=k: nc.vector.tensor_copy(out=xbf[:, k, :], in_=xT[:, k, :]),
                       r=[("xT", k)], w=[("xbf", k)])

            def ffn(l, i):
                xb_all = [("xbf", k) for k in range(KC)]
                if cut < 2:
                    return
                for j in range(FC):
                    pg, pu = newps(), newps()
                    mm_fm(ringA, ringA_t, KC, lambda kc: xbf[:, kc, :], xb_all, pg)
                    mm_fm(ringA, ringA_t, KC, lambda kc: xbf[:, kc, :], xb_all, pu)
                    b = j % 2
                    op("act", lambda pg=pg, b=b: nc.scalar.activation(out=sgt[:, b, :], in_=pss[pg][:, :], func=AF.Silu),
                       r=[("ps", pg)], w=[("sgt", b)])
                    op("dve", lambda pu=pu, b=b, j=j: nc.vector.tensor_tensor(out=hT[:, j, :], in0=sgt[:, b, :],
                                                                           in1=pss[pu][:, :], op=ALU.mult),
                       r=[("sgt", b), ("ps", pu)], w=[("hT", j)])
                h_all = [("hT", j) for j in range(FC)]
                if cut < 3:
                    return
                for dc in range(8):
                    pd = newps()
                    mm_fm(ringB, ringB_t, FC, lambda kc: hT[:, kc, :], h_all, pd)
                    op("dve", lambda pd=pd, dc=dc: nc.vector.scalar_tensor_tensor(
                        out=xT[:, dc, :], in0=pss[pd][:, :], scalar=float(0.5 / ALPHA), in1=xT[:, dc, :],
                        op0=ALU.mult, op1=ALU.add), r=[("ps", pd), ("xT", dc)], w=[("xT", dc)])
                    stats_acc(dc)
                if cut < 4:
                    return
                layer_norm(l, 2 * i)

            def stage_b():
                i = stn[0] % 4
                stn[0] += 1
                return i

            def stage_f():
                i = stn[1] % 4
                stn[1] += 1
                return i

            for t in range(NTILE):
                tsl = slice(t * T, (t + 1) * T)
                if first and "no:first" not in dbg:
                    op("sp", lambda t=t: nc.sync.dma_start(out=tok[:, :, :],
                                                           in_=x_in[t * T:(t + 1) * T, :].rearrange("(s p) d -> p s d", p=128)),
                       w=["tok"], dkey="tok")
                    for k in range(KC):
                        pi = newps()
                        for s in range(4):
                            op("pe", lambda k=k, s=s, pi=pi: nc.tensor.transpose(pss[pi][:, s * 128:(s + 1) * 128],
                                                                               tok[:, s, k * 128:(k + 1) * 128], ident_f[:, :]),
                               r=["tok"], w=[("ps", pi)])
                        op("dve", lambda k=k, pi=pi: nc.vector.tensor_copy(out=xT[:, k, :], in_=pss[pi][:, :]),
                           r=[("ps", pi)], w=[("xT", k)])
                        op("act", lambda k=k: nc.scalar.activation(out=xbf[:, k, :], in_=xT[:, k, :], func=AF.Identity),
                           r=[("xT", k)], w=[("xbf", k)])
                if do_D is not None:
                    l = do_D
                    op("sp", lambda tsl=tsl: nc.sync.dma_start(out=mixbf[:, :, :], in_=MIXT[:, :, tsl].rearrange("k p t -> p k t")),
                       w=["mixbf"], dkey="mixbf")
                    op("sp", lambda tsl=tsl: nc.sync.dma_start(out=xT[:, :, :], in_=X1T[:, :, tsl].rearrange("k p t -> p k t")),
                       w=[("xT", k) for k in range(KC)], dkey="x1ld")
                    for dc in range(8):
                        pd = newps()
                        mm_fm(ringA, ringA_t, KC, lambda kc: mixbf[:, kc, :], ["mixbf"], pd)
                        op("dve", lambda pd=pd, dc=dc: nc.vector.scalar_tensor_tensor(
                            out=xT[:, dc, :], in0=pss[pd][:, :], scalar=float(1.0 / ALPHA), in1=xT[:, dc, :],
                            op0=ALU.mult, op1=ALU.add), r=[("ps", pd), ("xT", dc)], w=[("xT", dc)])
                        stats_acc(dc)
                    layer_norm(l, 1)
                    ffn(l, 1)
                if last:
                    for s in range(4):
                        for half in range(2):
                            pi = newps()
                            for kk in range(4):
                                k = half * 4 + kk
                                op("pe", lambda k=k, kk=kk, s=s, pi=pi: nc.tensor.transpose(
                                    pss[pi][:, kk * 128:(kk + 1) * 128], xT[:, k, s * 128:(s + 1) * 128], ident_f[:, :]),
                                   r=[("xT", k)], w=[("ps", pi)])
                            op("dve", lambda s=s, half=half, pi=pi: nc.vector.tensor_copy(
                                out=tok[:, s, half * 512:(half + 1) * 512], in_=pss[pi][:, :]),
                               r=[("ps", pi)], w=[("tok", s, half)])
                    op("sp", lambda t=t: nc.sync.dma_start(out=y_out[t * T:(t + 1) * T, :].rearrange("(s p) d -> p s d", p=128),
                                                           in_=tok[:, :, :]),
                       r=[("tok", s, h_) for s in range(4) for h_ in range(2)], dkey="ystore")
                if do_A is not None:
                    l = do_A
                    ffn(l, 0)
                    if cut < 5:
                        break
                    op("sp", lambda tsl=tsl: nc.sync.dma_start(out=X1T[:, :, tsl].rearrange("k p t -> p k t"), in_=xT[:, :, :]),
                       r=[("xT", k) for k in range(KC)], dkey="x1st")
                    op("sp", lambda tsl=tsl: nc.sync.dma_start(out=ctt[:, 0, :], in_=CT[:, tsl]), w=[("ctt", 0)], dkey="ctl")
                    op("sp", lambda tsl=tsl: nc.sync.dma_start(out=ctt[:, 1, :], in_=ST[:, tsl]), w=[("ctt", 1)], dkey="stl")
                    xb_all = [("xbf", k) for k in range(KC)]
                    for e in range(32):
                        if cut < 6 + e:
                            break
                        if e < 8:
                            h = e % 4
                            qk = e // 4
                            pA, pB = newps(), newps()
                            s = ringA.get()
                            for kc in range(KC):
                                op("pe", lambda s=s, kc=kc, pA=pA: nc.tensor.matmul(
                                    pss[pA][:, :], lhsT=ringA_t[:, s, kc * 128:(kc + 1) * 128], rhs=xbf[:, kc, :],
                                    start=(kc == 0), stop=(kc == KC - 1)), r=[("rA", s), ("xbf", kc)], w=[("ps", pA)])
                            ringA.done()
                            s2 = ringA.get()
                            for kc in range(KC):
                                op("pe", lambda s2=s2, kc=kc, pB=pB: nc.tensor.matmul(
                                    pss[pB][:, :], lhsT=ringA_t[:, s2, kc * 128:(kc + 1) * 128], rhs=xbf[:, kc, :],
                                    start=(kc == 0), stop=(kc == KC - 1)), r=[("rA", s2), ("xbf", kc)], w=[("ps", pB)])
                            ringA.done()
                            sbi = stage_b()
                            op("dve", lambda pA=pA: nc.vector.tensor_tensor(out=rA[:, 0, :], in0=pss[pA][:, :], in1=ctt[:, 0, :],
                                                                           op=ALU.mult), r=[("ps", pA), ("ctt", 0)], w=[("rA_", 0)])
                            op("dve", lambda pB=pB: nc.vector.tensor_tensor(out=rA[:, 1, :], in0=pss[pB][:, :], in1=ctt[:, 1, :],
                                                                           op=ALU.mult), r=[("ps", pB), ("ctt", 1)], w=[("rA_", 1)])
                            op("dve", lambda sbi=sbi: nc.vector.tensor_tensor(out=stb[:, sbi, :], in0=rA[:, 0, :], in1=rA[:, 1, :],
                                                                             op=ALU.add), r=[("rA_", 0), ("rA_", 1)], w=[("stb", sbi)])
                            op("sp", lambda e=e, sbi=sbi, tsl=tsl: nc.sync.dma_start(out=QKT[e, :, tsl], in_=stb[:, sbi, :]),
                               r=[("stb", sbi)], dkey=("stb", sbi))
                            sfi = e % 2
                            op("act", lambda sbi=sbi, sfi=sfi: nc.scalar.activation(out=sqb[:, sfi, :],
                                                                                  in_=stb[:, sbi, :], func=AF.Square),
                               r=[("stb", sbi)], w=[("sqb", sfi)])
                            for c in range(2):
                                pn = newps()
                                op("pe", lambda sfi=sfi, c=c, pn=pn: nc.tensor.matmul(
                                    pss[pn][:, :], lhsT=blk[:, c, :], rhs=sqb[:, sfi, :],
                                    start=True, stop=True), r=[("sqb", sfi)], w=[("ps", pn)])
                                qi = ((l * 4 + h) * 2 + qk) * 2 + c
                                op("dve", lambda pn=pn, c=c: nc.vector.tensor_reduce(out=mx[:, c:c + 1], in_=pss[pn][:, :],
                                                                                   axis=AX.X, op=ALU.max),
                                   r=[("ps", pn)], w=[("mx", c)])
                                op("dve", lambda qi=qi, c=c: nc.vector.tensor_tensor(out=qkmax[:, qi:qi + 1], in0=qkmax[:, qi:qi + 1],
                                                                                   in1=mx[:, c:c + 1], op=ALU.max),
                                   r=[("mx", c), ("qkmax", qi)], w=[("qkmax", qi)])
                        elif 8 <= e < 12 or 24 <= e < 28:
                            h = e % 4
                            s = ringA.get()
                            pi = newps()
                            for sub in range(4):
                                for kc in range(KC):
                                    op("pe", lambda s=s, kc=kc, sub=sub, pi=pi: nc.tensor.matmul(
                                        pss[pi][:, sub * 128:(sub + 1) * 128], lhsT=xbf[:, kc, sub * 128:(sub + 1) * 128],
                                        rhs=ringA_t[:, s, kc * 128:(kc + 1) * 128], start=(kc == 0), stop=(kc == KC - 1)),
                                       r=[("rA", s)] + xb_all, w=[("ps", pi)])
                            ringA.done()
                            if e < 12:
                                op("dve", lambda pi=pi, h=h: nc.vector.tensor_copy(
                                    out=vst[:, :, h, 0:128], in_=pss[pi][:, :].rearrange("p (s c) -> p s c", s=4)),
                                   r=[("ps", pi), "vst"], w=[("vst", h)])
                                op("sp", lambda t=t, h=h: nc.sync.dma_start(
                                    out=VS[h, t * 4:(t + 1) * 4, :, :].rearrange("s p c -> p s c"), in_=vst[:, :, h, :]),
                                   r=[("vst", h)], dkey=("vst", h))
                            else:
                                op("dve", lambda pi=pi, h=h: nc.vector.tensor_copy(
                                    out=hist[:, :, h * 128:(h + 1) * 128], in_=pss[pi][:, :].rearrange("p (s c) -> p s c", s=4)),
                                   r=[("ps", pi)], w=[("hist", h)])
                                if h == 3:
                                    op("sp", lambda t=t: nc.sync.dma_start(
                                        out=HI[t * T:(t + 1) * T, :].rearrange("(s p) e -> p s e", p=128), in_=hist[:, :, :]),
                                       r=[("hist", hh) for hh in range(4)], dkey="hist")
                        else:
                            h = e % 4
                            pi = newps()
                            mm_fm(ringA, ringA_t, KC, lambda kc: xbf[:, kc, :], xb_all, pi)
                            if 12 <= e < 16 or e >= 28:
                                dst = HQ if e < 16 else GATE
                                sbi = stage_b()
                                op("act", lambda pi=pi, sbi=sbi: nc.scalar.activation(out=stb[:, sbi, :], in_=pss[pi][:, :], func=AF.Silu),
                                   r=[("ps", pi)], w=[("stb", sbi)])
                                op("sp", lambda dst=dst, h=h, sbi=sbi, tsl=tsl: nc.sync.dma_start(out=dst[h, :, tsl], in_=stb[:, sbi, :]),
                                   r=[("stb", sbi)], dkey=("stb", sbi))
                            else:
                                dst = ZF if e < 20 else ZB
                                sfi = stage_f()
                                op("dve", lambda pi=pi, sfi=sfi: nc.vector.tensor_copy(out=stf[:, sfi, :], in_=pss[pi][:, :]),
                                   r=[("ps", pi)], w=[("stf", sfi)])
                                op("sp", lambda dst=dst, h=h, sfi=sfi, tsl=tsl: nc.sync.dma_start(out=dst[h, :, tsl], in_=stf[:, sfi, :]),
                                   r=[("stf", sfi)], dkey=("stf", sfi))
            sch.barrier()

    def attention_phase(l):
        with ExitStack() as cs_:
            kt_t = sb(cs_, "kt_t", [128, 2, NT], BF16)
            qt_t = sb(cs_, "qt_t", [128, 2, NT], BF16)
            v_t = sb(cs_, "v_t", [128, 2, NKT, 129], BF16)
            pt = sb(cs_, "pt", [128, 2, 2 * T], BF16)
            biasT = sb(cs_, "biasT", [128, 2, 2, NKT * NTILE], F32)
            osb = sb(cs_, "osb", [128, 4, 2, 129], F32)
            rec = sb(cs_, "rec", [128, 2, 2], F32)
            t1 = sb(cs_, "t1", [128, 2, 128], F32)
            ob = sb(cs_, "ob", [128, 2, 128], F32)
            junk = sb(cs_, "junk", [128, 128], F32)
            ssq = sb(cs_, "ssq", [128, 2], F32)
            aob = sb(cs_, "aob", [128, 4, 128], BF16)
            aost = sb(cs_, "aost", [128, 2, T], BF16)
            mtmp = sb(cs_, "mtmp", [128, 2], F32)
            psS = [pst(cs_, "aS%d" % i, [128, 2 * T], F32) for i in range(2)]
            psO = [pst(cs_, "aO%d" % i, [128, T], F32) for i in range(3)]
            psT = pst(cs_, "aT", [128, T], BF16)
            a, v, g = nc.scalar, nc.vector, nc.gpsimd
            def head_loads(h):
                hb = h % 2
                op("sp", lambda h=h, hb=hb: nc.sync.dma_start(out=kt_t[:, hb, :], in_=QKT[4 + h, :, :]), w=[("kt", hb)], dkey=("kt", hb))
                op("sp", lambda h=h, hb=hb: nc.sync.dma_start(out=qt_t[:, hb, :], in_=QKT[h, :, :]), w=[("qt", hb)], dkey=("qt", hb))
                op("sp", lambda h=h, hb=hb: nc.sync.dma_start(out=v_t[:, hb, :, :], in_=VS[h, :, :, :].rearrange("k p c -> p k c")),
                   w=[("vt", hb)], dkey=("vt", hb))

            head_loads(0)
            for h in range(4):
                hb = h % 2
                if h + 1 < 4:
                    head_loads(h + 1)
                qi = ((l * 4 + h) * 2 + 0) * 2
                ki = ((l * 4 + h) * 2 + 1) * 2
                ni = (l * 4 + h) * 2
                op("dve", lambda qi=qi: v.tensor_tensor(out=mtmp[:, 0:1], in0=qkmax[:, qi:qi + 1], in1=qkmax[:, qi + 1:qi + 2], op=ALU.max),
                   w=[("mtmp", 0)])
                op("dve", lambda ki=ki: v.tensor_tensor(out=mtmp[:, 1:2], in0=qkmax[:, ki:ki + 1], in1=qkmax[:, ki + 1:ki + 2], op=ALU.max),
                   w=[("mtmp", 1)])
                op("dve", lambda: v.tensor_tensor(out=mtmp[:, 0:1], in0=mtmp[:, 0:1], in1=mtmp[:, 1:2], op=ALU.mult),
                   r=[("mtmp", 0), ("mtmp", 1)], w=[("mtmp", 0)])
                op("pool", lambda: g.tensor_tensor(out=mtmp[:, 0:1], in0=mtmp[:, 0:1], in1=mhalf[:, 0:1], op=ALU.pow),
                   r=[("mtmp", 0)], w=[("mtmp", 0)])
                op("dve", lambda: v.reciprocal(out=mtmp[:, 0:1], in_=mtmp[:, 0:1]), r=[("mtmp", 0)], w=[("mtmp", 0)])
                op("dve", lambda ni=ni: v.tensor_scalar(out=negM[:, ni:ni + 1], in0=mtmp[:, 0:1], scalar1=-0.125,
                                                       scalar2=None, op0=ALU.mult), r=[("mtmp", 0)], w=[("negM", ni)])
                op("dve", lambda ni=ni, hb=hb: v.tensor_scalar(out=biasT[:, hb, 0, :], in0=amask[:, :], scalar1=negM[:, ni:ni + 1],
                                                              scalar2=None, op0=ALU.add), r=[("negM", ni)], w=[("biasT", hb)])

                def emit_qk(qb, kt, hb=hb):
                    buf = kt % 2
                    ksl = slice(kt * 128, (kt + 1) * 128)
                    qsl = slice(qb * T, (qb + 1) * T)
                    for c in range(2):
                        rs = slice(c * 64, (c + 1) * 64)
                        op("pe", lambda buf=buf, c=c, rs=rs, ksl=ksl, qsl=qsl: nc.tensor.matmul(
                            psS[buf][:, c * T:(c + 1) * T], lhsT=kt_t[rs, hb, ksl], rhs=qt_t[rs, hb, qsl], start=True, stop=True),
                           r=[("kt", hb), ("qt", hb)], w=[("S", buf)])

                its = [(qb, kt) for qb in range(NTILE) for kt in range(NKT)]
                emit_qk(*its[0])
                if len(its) > 1:
                    emit_qk(*its[1])
                pending = []
                for ii, (qb, kt) in enumerate(its):
                    qsl = slice(qb * T, (qb + 1) * T)
                    buf = kt % 2
                    bi = kt * NTILE + qb
                    if pending and kt == min(6, NKT - 1):
                        for tl in pending:
                            tl()
                        pending = []
                    op("act", lambda buf=buf, bi=bi, hb=hb: a.activation(
                        out=pt[:, buf, :], in_=psS[buf][:, :], func=AF.Exp, scale=0.125, bias=biasT[:, hb, 0, bi:bi + 1]),
                       r=[("S", buf), ("biasT", hb)], w=[("pt", buf)])
                    if ii + 2 < len(its):
                        emit_qk(*its[ii + 2])
                    for c in range(2):
                        for s in range(4):
                            ai = c * 4 + s
                            bk, off = ai // 3, (ai % 3) * 129
                            op("pe", lambda buf=buf, c=c, s=s, bk=bk, off=off, kt=kt, ai=ai, hb=hb: nc.tensor.matmul(
                                psO[bk][:, off:off + 129], lhsT=pt[:, buf, c * T + s * 128:c * T + (s + 1) * 128], rhs=v_t[:, hb, kt, :],
                                start=(kt == 0 and ai % 3 == 0), stop=(kt == NKT - 1), skip_group_check=True),
                               r=[("pt", buf), ("vt", hb)], w=[("O", bk)])
                    if kt != NKT - 1:
                        continue
                    ab = qb % 2
                    for s in range(4):
                        for c in range(2):
                            ai = c * 4 + s
                            bk, off = ai // 3, (ai % 3) * 129
                            op("dve", lambda s=s, c=c, bk=bk, off=off: v.tensor_copy(out=osb[:, s, c, :], in_=psO[bk][:, off:off + 129]),
                               r=[("O", bk)], w=[("osb", s, c)])
                    for s in range(4):
                        sp_ = s % 2
                        op("dve", lambda sp_=sp_, s=s: v.reciprocal(out=rec[:, sp_, :], in_=osb[:, s, :, 128]),
                           r=[("osb", s, 0), ("osb", s, 1)], w=[("rec", sp_)])
                        op("dve", lambda sp_=sp_: v.tensor_scalar(out=rec[:, sp_, 1:2], in0=rec[:, sp_, 1:2], scalar1=lam[:, l:l + 1],
                                                                 scalar2=None, op0=ALU.mult), r=[("rec", sp_)], w=[("rec", sp_)])
                        op("dve", lambda sp_=sp_, s=s: v.tensor_scalar(out=t1[:, sp_, :], in0=osb[:, s, 1, 0:128], scalar1=rec[:, sp_, 1:2],
                                                                      scalar2=None, op0=ALU.mult), r=[("rec", sp_), ("osb", s, 1)], w=[("t1", sp_)])
                        op("dve", lambda sp_=sp_, s=s: v.scalar_tensor_tensor(out=ob[:, sp_, :], in0=osb[:, s, 0, 0:128], scalar=rec[:, sp_, 0:1],
                                                                             in1=t1[:, sp_, :], op0=ALU.mult, op1=ALU.subtract),
                           r=[("rec", sp_), ("osb", s, 0), ("t1", sp_)], w=[("ob", sp_)])
                        op("dve", lambda sp_=sp_: v.scalar_tensor_tensor(out=junk[:, :], in0=ob[:, sp_, :], scalar=1.0, in1=ob[:, sp_, :],
                                                                        op0=ALU.mult, op1=ALU.mult, accum_out=ssq[:, sp_:sp_ + 1]),
                           r=[("ob", sp_)], w=["junk", ("ssq", sp_)])
                        op("dve", lambda sp_=sp_: v.tensor_scalar(out=ssq[:, sp_:sp_ + 1], in0=ssq[:, sp_:sp_ + 1], scalar1=1.0 / 128.0,
                                                                 scalar2=EPS, op0=ALU.mult, op1=ALU.add), r=[("ssq", sp_)], w=[("ssq", sp_)])
                        op("pool", lambda sp_=sp_: g.tensor_tensor(out=ssq[:, sp_:sp_ + 1], in0=ssq[:, sp_:sp_ + 1], in1=mhalf[:, 0:1], op=ALU.pow),
                           r=[("ssq", sp_)], w=[("ssq", sp_)])
                        op("dve", lambda sp_=sp_, s=s: v.scalar_tensor_tensor(out=aob[:, s, :], in0=ob[:, sp_, :], scalar=ssq[:, sp_:sp_ + 1],
                                                                             in1=agb[:, l * 128:(l + 1) * 128], op0=ALU.mult, op1=ALU.mult),
                           r=[("ob", sp_), ("ssq", sp_)], w=[("aob", s)])

                    def tail(ab=ab, h=h, qsl=qsl):
                        for s in range(4):
                            op("pe", lambda s=s: nc.tensor.transpose(psT[:, s * 128:(s + 1) * 128], aob[:, s, :], ident_bf[:, :]),
                               r=[("aob", s)], w=["psT"])
                        op("dve", lambda: v.tensor_copy(out=aost[:, ab, :], in_=psT[:, :]), r=["psT"], w=[("aost", ab)])
                        op("sp", lambda: nc.sync.dma_start(out=MIXT[h, :, qsl], in_=aost[:, ab, :]),
                           r=[("aost", ab)], dkey=("aost", ab))
                    pending.append(tail)
                for tl in pending:
                    tl()
                pending = []
            sch.barrier()

    def hgrn_phase(l):
        with ExitStack() as cs_:
            a, v, g = nc.scalar, nc.vector, nc.gpsimd
            z_t = sb(cs_, "z_t", [128, 2, 4, T], F32)
            hq_t = sb(cs_, "hq_t", [128, 2, 4, T], BF16)
            hi_t = sb(cs_, "hi_t", [64, 2, 8, 512], BF16)
            gt_t = sb(cs_, "gt_t", [128, 2, 4, T], BF16)
            of_t = sb(cs_, "of_t", [128, 2, 4, T], F32)
            e_t = sb(cs_, "e_t", [128, 4, T], F32)
            l1_t = sb(cs_, "l1_t", [128, 4, T], F32)
            l2_t = sb(cs_, "l2_t", [128, 4, T], F32)
            b_t = sb(cs_, "b_t", [128, 4, T], F32)
            eb_t = sb(cs_, "eb_t", [128, 8, T], F32)
            w_t = sb(cs_, "w_t", [128, 4, T], F32)
            Qt = sb(cs_, "Qt", [128, 8, T], BF16)
            Kt = sb(cs_, "Kt", [128, 8, T], BF16)
            ATs = sb(cs_, "ATs", [64, 4, 64], BF16)
            Ktok = sb(cs_, "Ktok", [64, 4, 128], BF16)
            S_f = sb(cs_, "S_f", [128, 4, 128], F32)
            S_tmp = sb(cs_, "S_tmp", [128, 4, 128], F32)
            S_b = sb(cs_, "S_b", [128, 4, 128], BF16)
            osum = sb(cs_, "osum", [128, 2, T], F32)
            osq = sb(cs_, "osq", [128, 2, T], F32)
            rs_t = sb(cs_, "rs_t", [128, 2, T], F32)
            host = sb(cs_, "host", [128, 2, T], BF16)
            ofst = sb(cs_, "ofst", [128, 2, T], F32)
            psA = [pst(cs_, "hA%d" % i, [64, 512], F32) for i in range(1)]
            psK = [pst(cs_, "hK%d" % i, [64, 512], BF16) for i in range(1)]
            psU = [pst(cs_, "hU%d" % i, [128, 512], F32) for i in range(1)]
            psOo = [pst(cs_, "hO%d" % i, [128, T], F32) for i in range(4)]
            psR = pst(cs_, "hR", [128, T], F32)
            for d in range(2):
                Zs = ZF if d == 0 else ZB
                mask = maskf if d == 0 else maskb
                for h in range(4):
                    op("pool", lambda h=h: g.memset(S_f[:, h, :], 0.0), w=[("S_f", h)])
                    op("pool", lambda h=h: g.memset(S_b[:, h, :], 0.0), w=[("S_b", h)])
                tiles = list(range(NTILE)) if d == 0 else list(range(NTILE - 1, -1, -1))

                def emit_loads(ti, Zs=Zs, d=d, tiles=tiles):
                    t = tiles[ti]
                    tb = ti % 2
                    tsl = slice(t * T, (t + 1) * T)
                    op("sp", lambda tb=tb, tsl=tsl, Zs=Zs: nc.sync.dma_start(out=z_t[:, tb, :, :], in_=Zs[:, :, tsl].rearrange("h p t -> p h t")),
                       w=[("z_t", tb)], dkey=("z_t", tb))
                    op("sp", lambda tb=tb, tsl=tsl: nc.sync.dma_start(out=hq_t[:, tb, :, :], in_=HQ[:, :, tsl].rearrange("h p t -> p h t")),
                       w=[("hq_t", tb)], dkey=("hq_t", tb))
                    op("sp", lambda tb=tb, t=t: nc.sync.dma_start(out=hi_t[:, tb, :, :], in_=HI[t * T:(t + 1) * T, :].rearrange("(c p) e -> p c e", p=64)),
                       w=[("hi_t", tb)], dkey=("hi_t", tb))
                    if d == 1:
                        op("sp", lambda tb=tb, tsl=tsl: nc.sync.dma_start(out=gt_t[:, tb, :, :], in_=GATE[:, :, tsl].rearrange("h p t -> p h t")),
                           w=[("gt_t", tb)], dkey=("gt_t", tb))
                        op("sp", lambda tb=tb, tsl=tsl: nc.sync.dma_start(out=of_t[:, tb, :, :], in_=OFWD[:, :, tsl].rearrange("h p t -> p h t")),
                           w=[("of_t", tb)], dkey=("of_t", tb))

                def emit_prep(ti, h, k, d=d, tiles=tiles):
                    tb = ti % 2
                    li = (l * 2 + d) * 4 + h
                    z = z_t[:, tb, h, :]
                    if k == 0:
                        op("act", lambda z=z, h=h: a.activation(out=e_t[:, h, :], in_=z, func=AF.Exp, scale=-1.0),
                           r=[("z_t", tb)], w=[("e_t", h)])
                    elif k == 1:
                        op("act", lambda h=h, li=li: a.activation(out=l1_t[:, h, :], in_=e_t[:, h, :], func=AF.Ln,
                                                                 scale=lbt[:, li:li + 1], bias=1.0), r=[("e_t", h)], w=[("l1_t", h)])
                        op("act", lambda h=h: a.activation(out=l2_t[:, h, :], in_=e_t[:, h, :], func=AF.Ln, scale=1.0, bias=1.0),
                           r=[("e_t", h)], w=[("l2_t", h)])
                    elif k == 2:
                        op("dve", lambda h=h: v.tensor_tensor(out=l1_t[:, h, :], in0=l1_t[:, h, :], in1=l2_t[:, h, :], op=ALU.subtract),
                           r=[("l1_t", h), ("l2_t", h)], w=[("l1_t", h)])
                        op("pool", lambda h=h, z=z: g.tensor_tensor(out=w_t[:, h, :], in0=z, in1=l2_t[:, h, :], op=ALU.add),
                           r=[("z_t", tb), ("l2_t", h)], w=[("w_t", h)])
                    elif k == 3:
                        if d == 0:
                            op("dve", lambda h=h: v.tensor_tensor_scan(out=b_t[:, h, :], data0=rmask[:, :], data1=l1_t[:, h, :], initial=0.0,
                                                                      op0=ALU.mult, op1=ALU.add), r=[("l1_t", h)], w=[("b_t", h)])
                        else:
                            op("dve", lambda h=h: v.tensor_tensor_scan(out=b_t[:, h, ::-1], data0=rmask[:, :], data1=l1_t[:, h, ::-1], initial=0.0,
                                                                      op0=ALU.mult, op1=ALU.add), r=[("l1_t", h)], w=[("b_t", h)])
                    elif k == 4:
                        op("act", lambda tb=tb, h=h: a.activation(out=eb_t[:, tb * 4 + h, :], in_=b_t[:, h, :], func=AF.Exp),
                           r=[("b_t", h)], w=[("eb_t", tb, h)])
                        op("dve", lambda h=h: v.tensor_tensor(out=w_t[:, h, :], in0=w_t[:, h, :], in1=b_t[:, h, :], op=ALU.add),
                           r=[("w_t", h), ("b_t", h)], w=[("w_t", h)])
                    else:
                        op("dve", lambda h=h, tb=tb: v.tensor_tensor(out=Qt[:, tb * 4 + h, :], in0=hq_t[:, tb, h, :], in1=eb_t[:, tb * 4 + h, :], op=ALU.mult),
                           r=[("hq_t", tb), ("eb_t", tb, h)], w=[("Qt", tb, h)])
                        op("act", lambda tb=tb, h=h, li=li: a.activation(out=Kt[:, tb * 4 + h, :], in_=w_t[:, h, :], func=AF.Exp, scale=-1.0,
                                                                        bias=l1m[:, li:li + 1]), r=[("w_t", h)], w=[("Kt", tb, h)])

                emit_loads(0)
                for h in range(4):
                    for k in range(6):
                        emit_prep(0, h, k)
                for ti, t in enumerate(tiles):
                    tb = ti % 2
                    tsl = slice(t * T, (t + 1) * T)
                    if ti + 1 < NTILE:
                        emit_loads(ti + 1)
                    if NTILE >= 2 and ti == NTILE // 2:
                        for h in range(4):
                            op("dve", lambda h=h: v.tensor_scalar(out=S_f[:, h, :], in0=S_f[:, h, :], scalar1=keep[:, 0:1], scalar2=None,
                                                                 op0=ALU.mult), r=[("S_f", h)], w=[("S_f", h)])
                            op("dve", lambda h=h: v.tensor_copy(out=S_b[:, h, :], in_=S_f[:, h, :]), r=[("S_f", h)], w=[("S_b", h)])
                    chunks = range(8) if d == 0 else range(7, -1, -1)
                    for cix, c in enumerate(chunks):
                        csl = slice(c * 64, (c + 1) * 64)
                        cend = c * 64 + 63 if d == 0 else c * 64
                        for h in range(4):
                            op("pe", lambda tb=tb, h=h, csl=csl: nc.tensor.matmul(psA[0][:, h * 64:(h + 1) * 64], lhsT=Kt[:, tb * 4 + h, csl], rhs=Qt[:, tb * 4 + h, csl],
                                                                         start=True, stop=True), r=[("Kt", tb, h), ("Qt", tb, h)], w=["psA"])
                            op("pe", lambda tb=tb, h=h, csl=csl: nc.tensor.transpose(psK[0][:, h * 128:(h + 1) * 128], Kt[:, tb * 4 + h, csl], ident_bf[:, :]),
                               r=[("Kt", tb, h)], w=["psK"])
                        for h in range(4):
                            op("dve", lambda h=h, mask=mask: v.tensor_tensor(out=ATs[:, h, :], in0=psA[0][:, h * 64:(h + 1) * 64], in1=mask[:, :],
                                                                            op=ALU.mult), r=["psA"], w=[("ATs", h)])
                            op("act", lambda h=h: a.copy(out=Ktok[:, h, :], in_=psK[0][:, h * 128:(h + 1) * 128]), r=["psK"], w=[("Ktok", h)])
                        for h in range(4):
                            vch = hi_t[:, tb, c, h * 128:(h + 1) * 128]
                            op("pe", lambda h=h, csl=csl, vch=vch: nc.tensor.matmul(psOo[h][:, csl], lhsT=vch, rhs=ATs[:, h, :], start=True, stop=False),
                               r=[("ATs", h), ("hi_t", tb)], w=[("psO", h)])
                            op("pe", lambda h=h, vch=vch: nc.tensor.matmul(psU[0][:, h * 128:(h + 1) * 128], lhsT=Ktok[:, h, :], rhs=vch, start=True, stop=True),
                               r=[("Ktok", h), ("hi_t", tb)], w=["psU"])
                        for h in range(4):
                            op("pe", lambda tb=tb, h=h, csl=csl: nc.tensor.matmul(psOo[h][:, csl], lhsT=S_b[:, h, :], rhs=Qt[:, tb * 4 + h, csl], start=False, stop=True),
                               r=[("S_b", h), ("Qt", tb, h)], w=[("psO", h)])
                        for h in range(4):
                            op("dve", lambda h=h: v.tensor_tensor(out=S_tmp[:, h, :], in0=S_f[:, h, :], in1=psU[0][:, h * 128:(h + 1) * 128], op=ALU.add),
                               r=[("S_f", h), "psU"], w=[("S_tmp", h)])
                            op("dve", lambda tb=tb, h=h, cend=cend: v.tensor_scalar(out=S_f[:, h, :], in0=S_tmp[:, h, :], scalar1=eb_t[:, tb * 4 + h, cend:cend + 1],
                                                                            scalar2=None, op0=ALU.mult), r=[("S_tmp", h), ("eb_t", tb, h)], w=[("S_f", h)])
                            op("act", lambda h=h: a.copy(out=S_b[:, h, :], in_=S_f[:, h, :]), r=[("S_f", h)], w=[("S_b", h)])
                        if ti + 1 < NTILE:
                            for h in range(4):
                                k = cix - (h // 2)
                                if 0 <= k <= 5:
                                    emit_prep(ti + 1, h, k)
                    for h in range(4):
                        hb = h % 2
                        if d == 0:
                            op("act", lambda h=h, hb=hb: a.copy(out=ofst[:, hb, :], in_=psOo[h][:, :]), r=[("psO", h)], w=[("ofst", hb)])
                            op("sp", lambda h=h, hb=hb, tsl=tsl: nc.sync.dma_start(out=OFWD[h, :, tsl], in_=ofst[:, hb, :]),
                               r=[("ofst", hb)], dkey=("ofst", hb))
                        else:
                            op("dve", lambda h=h, hb=hb, tb=tb: v.tensor_tensor(out=osum[:, hb, :], in0=psOo[h][:, :], in1=of_t[:, tb, h, :], op=ALU.add),
                               r=[("psO", h), ("of_t", tb)], w=[("osum", hb)])
                            op("act", lambda hb=hb: a.activation(out=osq[:, hb, :], in_=osum[:, hb, :], func=AF.Square),
                               r=[("osum", hb)], w=[("osq", hb)])
                            op("pe", lambda hb=hb: nc.tensor.matmul(psR[:, :], lhsT=ones128[:, :], rhs=osq[:, hb, :], start=True, stop=True),
                               r=[("osq", hb)], w=["psR"])
                            op("act", lambda hb=hb: a.activation(out=rs_t[:, hb, :], in_=psR[:, :], func=AF.Sqrt, bias=EPS, scale=1.0),
                               r=["psR"], w=[("rs_t", hb)])
                            op("dve", lambda hb=hb: v.reciprocal(out=rs_t[:, hb, :], in_=rs_t[:, hb, :]), r=[("rs_t", hb)], w=[("rs_t", hb)])
                            op("dve", lambda hb=hb: v.scalar_tensor_tensor(out=osum[:, hb, :], in0=osum[:, hb, :], scalar=hgg[:, l:l + 1],
                                                                          in1=rs_t[:, hb, :], op0=ALU.mult, op1=ALU.mult),
                               r=[("osum", hb), ("rs_t", hb)], w=[("osum", hb)])
                            op("dve", lambda hb=hb, h=h, tb=tb: v.tensor_tensor(out=host[:, hb, :], in0=osum[:, hb, :], in1=gt_t[:, tb, h, :], op=ALU.mult),
                               r=[("osum", hb), ("gt_t", tb)], w=[("host", hb)])
                            op("sp", lambda h=h, hb=hb, tsl=tsl: nc.sync.dma_start(out=MIXT[4 + h, :, tsl], in_=host[:, hb, :]),
                               r=[("host", hb)], dkey=("host", hb))
                sch.barrier()

    stop_after = [x_ for x_ in dbg if isinstance(x_, str) and x_.startswith("stop:")]
    stop_after = stop_after[0][5:] if stop_after else None
    seq = [("A0", lambda: chain_phase(None, 0, True, False)), ("B0", lambda: attention_phase(0)), ("C0", lambda: hgrn_phase(0)),
           ("DA", lambda: chain_phase(0, 1, False, False)), ("B1", lambda: attention_phase(1)), ("C1", lambda: hgrn_phase(1)),
           ("D1", lambda: chain_phase(1, None, False, True))]
    for name, fn in seq:
        if stop_after == "P0":
            break
        fn()
        if stop_after == name:
            break
    sch.barrier()
    es.close()
    return nc


def make_in_maps(inputs, NT, seq_lens):
    raise NotImplementedError


_NC_CACHE = {}


def core_aux(NT, nseq):
    NTILE = NT // T
    NKT = NT // 128
    sl = NT // nseq
    pos = (np.arange(NT) % sl).astype(np.float32)[None, :]
    am = np.zeros((NKT, NTILE), np.float32)
    for kt in range(NKT):
        for qb in range(NTILE):
            if (kt * 128) // sl != (qb * T) // sl:
                am[kt, qb] = NEG
    keep = np.array([[1.0 if nseq == 1 else 0.0]], np.float32)
    return pos, am.reshape(1, -1), keep


def kernel(x_prompt, x_sample, w_in, w_out, attn_lambda, attn_norm_g, hg_norm_g, hg_lower_bound,
           ffn_w_gate, ffn_w_up, ffn_w_down, ln_g, ln_b):
    NT = 8192
    if NT not in _NC_CACHE:
        _NC_CACHE[NT] = build(NT)
    nc = _NC_CACHE[NT]
    f = lambda a_: np.ascontiguousarray(np.asarray(a_, dtype=np.float32))
    shared = dict(w_in=f(w_in), w_out=f(w_out), attn_lambda=f(attn_lambda), attn_norm_g=f(attn_norm_g),
                  hg_norm_g=f(hg_norm_g), hg_lower_bound=f(hg_lower_bound), ffn_w_gate=f(ffn_w_gate),
                  ffn_w_up=f(ffn_w_up), ffn_w_down=f(ffn_w_down), ln_g=f(ln_g), ln_b=f(ln_b))
    xp = f(x_prompt)
    xs = f(x_sample)
    in_maps = []
    for c in range(8):
        if c < 4:
            xc = xp[2 * c:2 * c + 2].reshape(NT, D)
            pos, am, keep = core_aux(NT, 2)
        else:
            xc = xs[c - 4].reshape(NT, D)
            pos, am, keep = core_aux(NT, 1)
        m = dict(shared)
        m.update(x=np.ascontiguousarray(xc), pos=pos, amask=am, keep=keep)
        in_maps.append(m)
    res = run_bass_kernel_spmd(nc, in_maps, core_ids=list(range(8)))
    yp = np.stack([res.results[c]["y"].reshape(2, 4096, D) for c in range(4)]).reshape(8, 4096, D)
    ys = np.stack([res.results[c]["y"].reshape(8192, D) for c in range(4, 8)])
    return (yp.astype(np.float32), ys.astype(np.float32))
```
